# Optimizing a Trainium2 kernel written in Bass

```python
import math
import jax, jax.numpy as jnp
from jax import lax
import numpy as np

D_MODEL = 4096
BATCH = 2
SEQ = 8192
DEPTH = 1

CONV_CH = D_MODEL // 2
CONV_KERNEL = 31
N_HEADS = 16
HEAD_DIM = 128
ATTN_WIDTH = N_HEADS * HEAD_DIM
IDX_HEADS = 32
IDX_DIM = 64
TOPK_MAX = 256
Q_BLOCK = 128
N_BUCKETS = 32
MAX_EXACT = N_BUCKETS // 2
MAX_DISTANCE = 128
D_FF = 4 * D_MODEL
N_BRANCHES = 2
EPS = 1e-6

OFF_CONV_A = 0
OFF_CONV_G = OFF_CONV_A + CONV_CH
OFF_Q = OFF_CONV_G + CONV_CH
OFF_K = OFF_Q + ATTN_WIDTH
OFF_V = OFF_K + ATTN_WIDTH
OFF_IQ = OFF_V + ATTN_WIDTH
OFF_IK = OFF_IQ + IDX_HEADS * IDX_DIM
OFF_IW = OFF_IK + IDX_DIM
OFF_GATE = OFF_IW + IDX_HEADS
IN_WIDTH = OFF_GATE + N_BRANCHES * D_MODEL

kernel_name = "hybrid_conformer_conv_dsa_gated_block"


def _rmsnorm(x, g):
    xf = x.astype(jnp.float32)
    y = xf * lax.rsqrt(jnp.mean(xf * xf, axis=-1, keepdims=True) + EPS)
    return (y * g.astype(jnp.float32)).astype(x.dtype)


def _layernorm(x, g, b):
    xf = x.astype(jnp.float32)
    mu = jnp.mean(xf, axis=-1, keepdims=True)
    xc = xf - mu
    y = xc * lax.rsqrt(jnp.mean(xc * xc, axis=-1, keepdims=True) + EPS)
    return (y * g.astype(jnp.float32) + b.astype(jnp.float32)).astype(x.dtype)


def _t5_bucket(n):
    nf = jnp.maximum(n, 1).astype(jnp.float32)
    large = MAX_EXACT + (jnp.log(nf / MAX_EXACT) / math.log(MAX_DISTANCE / MAX_EXACT)
                         * (N_BUCKETS - MAX_EXACT)).astype(jnp.int32)
    large = jnp.minimum(large, N_BUCKETS - 1)
    return jnp.where(n < MAX_EXACT, n, large)


def _conv_branch(a, g, conv_w, conv_bias, ln_g, ln_b, w_out):
    h = a * jax.nn.sigmoid(g)
    h = lax.conv_general_dilated(
        h, conv_w[:, None, :].astype(h.dtype), window_strides=(1,),
        padding=[(CONV_KERNEL - 1, 0)], dimension_numbers=('NWC', 'WIO', 'NWC'),
        feature_group_count=CONV_CH) + conv_bias
    h = jax.nn.silu(_layernorm(h, ln_g, ln_b))
    return h @ w_out


def _dsa_branch(q, k, v, qi, ki, wi, rel_bias, w_out):
    B, S = q.shape[0], q.shape[1]
    topk = min(TOPK_MAX, S // 4)
    nb = S // Q_BLOCK
    key_pos = jnp.arange(S, dtype=jnp.int32)
    wi = wi * (IDX_HEADS ** -0.5)

    def to_blocks(t):
        return jnp.moveaxis(t.reshape((B, nb, Q_BLOCK) + t.shape[2:]), 1, 0)

    def block(args):
        qb, qib, wib, start = args
        q_pos = start + jnp.arange(Q_BLOCK, dtype=jnp.int32)
        s = jnp.einsum('bqhd,bsd->bqhs', qib, ki) * (IDX_DIM ** -0.5)
        score = jnp.einsum('bqhs,bqh->bqs', jax.nn.relu(s), wib).astype(jnp.float32)
        causal = key_pos[None, :] <= q_pos[:, None]
        score = jnp.where(causal[None], score, -jnp.inf)
        _, sel = lax.top_k(score, topk)
        k_sel = jax.vmap(lambda kb, ib: kb[ib])(k, sel)
        v_sel = jax.vmap(lambda vb, ib: vb[ib])(v, sel)
        logits = jnp.einsum('bqhd,bqkhd->bhqk', qb, k_sel).astype(jnp.float32) * (HEAD_DIM ** -0.5)
        dist = q_pos[None, :, None] - sel
        bias = rel_bias[_t5_bucket(jnp.maximum(dist, 0))]
        logits = logits + jnp.transpose(bias, (0, 3, 1, 2)).astype(jnp.float32)
        logits = jnp.where((dist >= 0)[:, None], logits, -jnp.inf)
        p = jax.nn.softmax(logits, axis=-1).astype(v.dtype)
        return jnp.einsum('bhqk,bqkhd->bqhd', p, v_sel)

    starts = jnp.arange(nb, dtype=jnp.int32) * Q_BLOCK
    o = lax.map(block, (to_blocks(q), to_blocks(qi), to_blocks(wi), starts))
    o = jnp.moveaxis(o, 0, 1).reshape(B, S, ATTN_WIDTH)
    return o @ w_out


def setup_inputs(seed: int = 0) -> dict:
    key = jax.random.key(seed)
    ks = jax.random.split(key, 20)
    f32 = jnp.float32
    nrm = lambda k, shape, scale: jax.random.normal(k, shape, f32) * scale
    L = DEPTH
    return {
        "x": nrm(ks[0], (BATCH, SEQ, D_MODEL), 1.0),
        "norm1_g": 1.0 + nrm(ks[1], (L, D_MODEL), 0.05),
        "w_in": nrm(ks[2], (L, D_MODEL, IN_WIDTH), D_MODEL ** -0.5),
        "b_gate": nrm(ks[3], (L, N_BRANCHES * D_MODEL), 0.01),
        "conv_w": nrm(ks[4], (L, CONV_KERNEL, CONV_CH), CONV_KERNEL ** -0.5),
        "conv_bias": nrm(ks[5], (L, CONV_CH), 0.01),
        "conv_ln_g": 1.0 + nrm(ks[6], (L, CONV_CH), 0.05),
        "conv_ln_b": nrm(ks[7], (L, CONV_CH), 0.01),
        "w_conv_out": nrm(ks[8], (L, CONV_CH, D_MODEL), CONV_CH ** -0.5),
        "w_attn_out": nrm(ks[9], (L, ATTN_WIDTH, D_MODEL), ATTN_WIDTH ** -0.5),
        "rel_bias": nrm(ks[10], (N_BUCKETS, N_HEADS), 0.5),
        "w_o": nrm(ks[11], (L, D_MODEL, D_MODEL), D_MODEL ** -0.5),
        "norm2_g": 1.0 + nrm(ks[12], (L, D_MODEL), 0.05),
        "w_ff1": nrm(ks[13], (L, D_MODEL, D_FF), D_MODEL ** -0.5),
        "w_ff2": nrm(ks[14], (L, D_FF, D_MODEL), D_FF ** -0.5),
        "normf_g": 1.0 + nrm(ks[15], (D_MODEL,), 0.05),
    }


def reference(x, norm1_g, w_in, b_gate, conv_w, conv_bias, conv_ln_g, conv_ln_b,
              w_conv_out, w_attn_out, rel_bias, w_o, norm2_g, w_ff1, w_ff2, normf_g):
    B, S, _ = x.shape
    h = x
    for l in range(DEPTH):
        u = _rmsnorm(h, norm1_g[l])
        p = u @ w_in[l]
        conv_a = p[..., OFF_CONV_A:OFF_CONV_G]
        conv_g = p[..., OFF_CONV_G:OFF_Q]
        q = p[..., OFF_Q:OFF_K].reshape(B, S, N_HEADS, HEAD_DIM)
        k = p[..., OFF_K:OFF_V].reshape(B, S, N_HEADS, HEAD_DIM)
        v = p[..., OFF_V:OFF_IQ].reshape(B, S, N_HEADS, HEAD_DIM)
        qi = p[..., OFF_IQ:OFF_IK].reshape(B, S, IDX_HEADS, IDX_DIM)
        ki = p[..., OFF_IK:OFF_IW]
        wi = p[..., OFF_IW:OFF_GATE]
        gates = jax.nn.sigmoid(p[..., OFF_GATE:] + b_gate[l]).reshape(B, S, N_BRANCHES, D_MODEL)

        y_a = _conv_branch(conv_a, conv_g, conv_w[l], conv_bias[l], conv_ln_g[l], conv_ln_b[l], w_conv_out[l])
        y_b = _dsa_branch(q, k, v, qi, ki, wi, rel_bias, w_attn_out[l])
        mixed = gates[:, :, 0, :] * y_a + gates[:, :, 1, :] * y_b
        h = h + mixed @ w_o[l]

        u2 = _rmsnorm(h, norm2_g[l])
        h = h + jnp.square(jax.nn.relu(u2 @ w_ff1[l])) @ w_ff2[l]
    return _rmsnorm(h, normf_g)
```

```python
import contextlib
import math
import numpy as np
import concourse.bass as bass
import concourse.mybir as mybir
from concourse.bass_utils import run_bass_kernel_spmd

F32 = mybir.dt.float32
BF16 = mybir.dt.bfloat16
FP16 = mybir.dt.float16
I16 = mybir.dt.int16
AF = mybir.ActivationFunctionType
ALU = mybir.AluOpType
AX = mybir.AxisListType

ENGS = ("pe", "act", "dve", "pool", "sp")
DMA_RING = 24
EPS = 1e-6
NITER = 24
NEG = -1.0e30


class Cfg:
    def __init__(self, D=4096, S=8192, H=16, IH=32, DFF=16384, TOPK=256):
        self.D, self.S, self.H, self.IH, self.DFF, self.TOPK = D, S, H, IH, DFF, TOPK
        self.CH = D // 2
        self.HW = H * 128
        self.IW = IH * 64
        self.DC = D // 128
        self.CC = self.CH // 128
        self.TOWN = S // 4
        self.NSC = S // 128
        self.OFF_A = 0
        self.OFF_G = self.CH
        self.OFF_Q = 2 * self.CH
        self.OFF_K = self.OFF_Q + self.HW
        self.OFF_V = self.OFF_K + self.HW
        self.OFF_IQ = self.OFF_V + self.HW
        self.OFF_IK = self.OFF_IQ + self.IW
        self.OFF_IW = self.OFF_IK + 64
        self.OFF_GATE = self.OFF_IW + IH
        self.INW = self.OFF_GATE + 2 * D
        self.G = min(512, self.TOWN)
        self.QB = min(512, self.TOWN)
        self.GF = min(256, self.TOWN)
        self.HALO_CHUNK = self.TOWN // 128
        self.PW1 = 256
        self.KP = min(16, DFF // 128)


class Dep:
    __slots__ = ("w", "r", "ro")

    def __init__(self, ro=False):
        self.w = None
        self.r = {}
        self.ro = ro


class Sched:
    def __init__(self, nc, stack):
        self.nc = nc
        self.ops = {e: [] for e in ENGS}
        self.cnt = {e: 0 for e in ENGS}
        self.seen = {e: {} for e in ENGS}
        self.sem = {e: stack.enter_context(nc.semaphore("s_" + e)) for e in ENGS}
        self.ring = {}
        self.dma_n = {}
        for q in ("sp", "pool", "act"):
            self.ring[q] = [stack.enter_context(nc.semaphore(f"d_{q}{i}")) for i in range(DMA_RING)]
            self.dma_n[q] = 0
        self.nops = 0

    def _need(self, eng, waits, cid):
        if cid is None:
            return
        sem, val, src = cid
        if src == eng and src == "pe":
            return
        key = id(sem)
        if self.seen[eng].get(key, 0) >= val:
            return
        self.seen[eng][key] = val
        waits.append((sem, val))

    def op(self, eng, fn, reads=(), writes=(), dma=False):
        waits = []
        for d in reads:
            self._need(eng, waits, d.w)
        for d in writes:
            self._need(eng, waits, d.w)
            for r in d.r.values():
                self._need(eng, waits, r)
        if dma:
            n = self.dma_n[eng]
            self.dma_n[eng] = n + 1
            sem = self.ring[eng][n % DMA_RING]
            k = n // DMA_RING
            if k > 0:
                self._need(eng, waits, (sem, 16 * k, "dma"))
            cid = (sem, 16 * (k + 1), "dma")
            inc = (sem, 16)
            rkey = id(sem)
        else:
            self.cnt[eng] += 1
            cid = (self.sem[eng], self.cnt[eng], eng)
            inc = (self.sem[eng], 1)
            rkey = eng
        self.ops[eng].append((waits, fn, inc))
        self.nops += 1
        for d in reads:
            if not d.ro:
                d.r[rkey] = cid
        for d in writes:
            d.w = cid
            d.r = {}
        return cid

    def barrier(self):
        ids = []
        for e in ENGS:
            if self.cnt[e] > 0:
                ids.append((self.sem[e], self.cnt[e], e))
        for q in self.ring:
            n = self.dma_n[q]
            for i in range(DMA_RING):
                k = (n - i + DMA_RING - 1) // DMA_RING if n > i else 0
                if k > 0:
                    ids.append((self.ring[q][i], 16 * k, "dma"))
        for e in ENGS:
            waits = []
            for cid in ids:
                if cid[2] == e:
                    continue
                self._need(e, waits, cid)
            if waits:
                self.ops[e].append((waits, None, None))

    def emit(self, block):
        emap = {"pe": block.tensor, "act": block.scalar, "dve": block.vector, "pool": block.gpsimd, "sp": block.sync}
        for e in ENGS:
            ops = self.ops[e]

            def body(eng, ops=ops):
                for waits, fn, inc in ops:
                    for sem, val in waits:
                        eng.wait_ge(sem, val)
                    if fn is not None:
                        fn(eng).then_inc(inc[0], inc[1])

            emap[e](body)


class WBuf:
    __slots__ = ("ap", "ds", "used")

    def __init__(self, ap):
        self.ap = ap
        self.ds = [Dep() for _ in range(3)]
        self.used = []


class Tl:
    __slots__ = ("ap", "d")

    def __init__(self, ap, d=None):
        self.ap = ap
        self.d = d if d is not None else Dep()


ARENA_F32 = 47 * 1024


class Builder:
    def __init__(self, cfg, debug=False):
        self.cfg = cfg
        self.debug = debug

    def reset_arena(self):
        self.off = self.const_off

    def af32(self, n):
        a = self.off
        self.off += n
        assert self.off <= ARENA_F32, ("SBUF overflow", self.off)
        return self.arena[:, a:a + n]

    def abf(self, n):
        n2 = (n + 1) // 2
        return self.af32(n2).bitcast(BF16)[:, 0:n]

    def tf32(self, n):
        return Tl(self.af32(n))

    def tbf(self, n):
        return Tl(self.abf(n))

    def ps_next(self):
        b = self.ps_i % 8
        self.ps_i += 1
        return self.ps[b]

    def dma(self, q, out, in_, reads=(), writes=()):
        return self.S.op(q, lambda e: e.dma_start(out=out, in_=in_), reads=reads, writes=writes, dma=True)

    def build(self):
        c = self.cfg
        nc = bass.Bass("TRN2", target_bir_lowering=False)
        self.nc = nc
        D, S, H, IH, DFF = c.D, c.S, c.H, c.IH, c.DFF
        dt_in = lambda name, shape: nc.dram_tensor(name, shape, F32, kind="ExternalInput").ap()
        self.xk = dt_in("xk", [S, D])
        self.kpos_in = nc.dram_tensor("kpos", [128, S], I16, kind="ExternalInput").ap()
        self.qpos_in = dt_in("qpos", [128, c.TOWN // 128])
        self.w_in = dt_in("w_in", [D, c.INW])
        self.w_co = dt_in("w_co", [c.CH, D])
        self.w_ao = dt_in("w_ao", [c.HW, D])
        self.w_o = dt_in("w_o", [D, D])
        self.w_1 = dt_in("w_1", [D, DFF])
        self.w_2 = dt_in("w_2", [DFF, D])
        self.g1_in = dt_in("g1", [128, c.DC])
        self.g2_in = dt_in("g2", [128, c.DC])
        self.gf_in = dt_in("gf", [128, D])
        self.bg_in = dt_in("bg", [128, 2 * c.DC])
        self.cw_in = dt_in("cw", [128, c.CC * 31])
        self.cb_in = dt_in("cb", [128, c.CC])
        self.lg_in = dt_in("lg", [128, c.CC])
        self.lb_in = dt_in("lb", [128, c.CC])
        self.rb31_in = dt_in("rb31", [128, H])
        self.ttab_in = dt_in("ttab", [128, H * 2 * 128])
        self.ident_in = dt_in("ident", [128, 128])
        self.out = nc.dram_tensor("out", [c.TOWN, D], F32, kind="ExternalOutput").ap()
        kind = "ExternalOutput" if self.debug else "Internal"
        di = lambda name, shape, dt: nc.dram_tensor(name, shape, dt, kind=kind).ap()
        self.KT = di("KT", [H, 128, S], BF16)
        self.V = di("V", [H, 128, S], BF16)
        self.kiT = di("kiT", [128, S], FP16)
        self.QT = di("QT", [H, 128, c.TOWN], BF16)
        self.qiT = di("qiT", [IH // 2, 128, c.TOWN], FP16)
        self.wi = di("wi", [c.TOWN, IH], F32)
        self.gluT = di("gluT", [c.CC, 128, 32 + c.TOWN], F32)
        self.gatesT = di("gatesT", [2 * c.DC, 128, c.TOWN], F32)
        self.zT = di("zT", [c.CC, 128, c.TOWN], BF16)
        self.maskT = di("maskT", [c.NSC, 128, c.TOWN], BF16)
        self.oT = di("oT", [H, 128, c.TOWN], BF16)
        self.h1 = di("h1", [c.TOWN, D], F32)

        with contextlib.ExitStack() as st:
            self.arena = st.enter_context(nc.sbuf_tensor("arena", [128, ARENA_F32], F32))
            self.ps = [Tl(st.enter_context(nc.psum_tensor(f"ps{i}", [128, 512], F32))) for i in range(8)]
            self.ps_i = 0
            self.S = Sched(nc, st)
            self.off = 0
            self.const_off = 0
            self.consts()
            self.const_off = self.off
            self.setup_panels()
            self._cast_g = self.cast_gen(["in"])
            c_ = self.cfg
            n_up = ((c_.H + 3) // 4 + (c_.HW + 511) // 512 + 1)
            self.pump_casts(n_up * ((c_.DC + 7) // 8) + 2 * ((c_.DC + 7) // 8))
            self.phase_inproj()
            self.pump_casts(100000)
            self.S.barrier()
            self.phase_conv()
            self.S.barrier()
            self._cast_g = self.cast_gen(["co", "ao", "o"])
            self.phase_indexer()
            self.pump_casts(100000)
            self.S.barrier()
            self._cast_g = self.cast_gen(["1", "2"])
            self.phase_attn()
            self.pump_casts(100000)
            self.S.barrier()
            self.phase_mix()
            self.S.barrier()
            self.phase_ffn()
            self.S.barrier()
            with nc.Block() as block:
                self.S.emit(block)
        return nc

    def consts(self):
        c = self.cfg
        ld = lambda n, src: self._ld_const(n, src)
        self.ident = ld(128, self.ident_in)
        self.g1 = ld(c.DC, self.g1_in)
        self.g2 = ld(c.DC, self.g2_in)
        self.bg = ld(2 * c.DC, self.bg_in)
        self.cw = ld(c.CC * 31, self.cw_in)
        self.cb = ld(c.CC, self.cb_in)
        self.lg = ld(c.CC, self.lg_in)
        self.lb = ld(c.CC, self.lb_in)
        self.rb31 = ld(c.H, self.rb31_in)
        self.qpos = ld(c.TOWN // 128, self.qpos_in)
        self.ones32 = self.tf32(128)
        self.S.op("pool", lambda e: e.memset(self.ones32.ap, 1.0), writes=[self.ones32.d])
        self.zbf = self.tbf(512)
        self.S.op("pool", lambda e: e.memset(self.zbf.ap, 0.0), writes=[self.zbf.d])

    def _ld_const(self, n, src):
        t = self.tf32(n)
        self.dma("sp", t.ap, src, writes=[t.d])
        return t

    def panel_lists(self):
        c = self.cfg
        L = {}
        li = []
        for i in range(0, c.H, 4):
            li.append(([(c.OFF_K + i * 128, min(4, c.H - i) * 128)], c.DC, 0))
        for i in range(0, c.HW, 512):
            li.append(([(c.OFF_V + i, min(512, c.HW - i))], c.DC, 0))
        li.append(([(c.OFF_IK, 64), (c.OFF_IK, 64), (c.OFF_IW, c.IH)], c.DC, 0))
        for i in range(0, c.CC, 2):
            ncc = min(2, c.CC - i)
            li.append(([(c.OFF_A + i * 128, ncc * 128), (c.OFF_G + i * 128, ncc * 128)], c.DC, 0))
        for i in range(0, c.H, 4):
            li.append(([(c.OFF_Q + i * 128, min(4, c.H - i) * 128)], c.DC, 0))
        for i in range(0, c.IH // 2, 4):
            li.append(([(c.OFF_IQ + i * 128, min(4, c.IH // 2 - i) * 128)], c.DC, 0))
        for i in range(0, 2 * c.DC, 4):
            li.append(([(c.OFF_GATE + i * 128, 512)], c.DC, 0))
        L["in"] = li
        L["co"] = [([(n * 512, 512)], c.CC, 0) for n in range(c.D // 512)]
        L["ao"] = [([(n * 512, 512)], c.H, 0) for n in range(c.D // 512)]
        L["o"] = [([(n * 512, 512)], c.DC, 0) for n in range(c.D // 512)]
        L["1"] = [([(n * c.PW1, c.PW1)], c.DC, 0) for n in range(c.DFF // c.PW1)]
        L["2"] = [([(n * 512, 512)], c.KP, kp * c.KP) for n in range(c.D // 512) for kp in range(c.DFF // 128 // c.KP)]
        return L

    def setup_panels(self):
        self.plist = self.panel_lists()
        self.preg = {}
        self.wbp = {}
        self.pdeps = {}
        self.wsrc = {"in": self.w_in, "co": self.w_co, "ao": self.w_ao, "o": self.w_o, "1": self.w_1, "2": self.w_2}
        for name, li in self.plist.items():
            mx = max(kc * sum(w for _, w in segs) for segs, kc, c0 in li)
            self.wbp[name] = self.nc.dram_tensor("wbp_" + name, [len(li), 128, mx], BF16, kind="Internal").ap()
            for idx, (segs, kc, c0) in enumerate(li):
                self.preg[(name, tuple(segs), kc, c0)] = idx

    def cast_weights(self, names):
        for _ in self.cast_gen(names):
            pass

    def pump_casts(self, n):
        g = getattr(self, "_cast_g", None)
        if g is None:
            return
        for _ in range(n):
            try:
                next(g)
            except StopIteration:
                self._cast_g = None
                return

    def cast_gen(self, names):
        for name in names:
            src = self.wsrc[name]
            for idx, (segs, kc, c0) in enumerate(self.plist[name]):
                pw = sum(w for _, w in segs)
                dstv = self.wbp[name][idx, :, 0:kc * pw].rearrange("p (c w) -> p c w", c=kc)
                deps = []
                o = 0
                for col0, w in segs:
                    for cb in range(0, kc, 8):
                        ce = min(kc, cb + 8)
                        d = Dep(ro=True)
                        self.dma("pool", dstv[:, cb:ce, o:o + w],
                                 src[(c0 + cb) * 128:(c0 + ce) * 128, col0:col0 + w].rearrange("(c p) w -> p c w", p=128), writes=[d])
                        deps.append(d)
                        self.pdeps[(name, idx)] = deps
                        yield
                    o += w
                self.pdeps[(name, idx)] = deps

    def load_panel(self, wb, wname, buf, segs, kc, c0=0):
        idx = self.preg[(wname, tuple(segs), kc, c0)]
        pw = sum(w for _, w in segs)
        prev = list(buf.used)
        buf.used = [buf.ds[0]]
        self.dma("sp", buf.ap[:, 0:kc * pw], self.wbp[wname][idx, :, 0:kc * pw], reads=self.pdeps[(wname, idx)],
                 writes=[buf.ds[0]] + [d for d in prev if d is not buf.ds[0]])
        return buf.ap[:, 0:kc * pw].rearrange("p (c n) -> p c n", c=kc)

    def norm_transpose(self, xt, u, junk, uT3, col0, gam, small):
        c = self.cfg
        S_ = self.S
        D = c.D
        ss, ms, sd, rstd = small
        S_.op("act", lambda e: e.activation(out=junk.ap, in_=xt.ap, func=AF.Square, accum_out=ss.ap),
              reads=[xt.d], writes=[junk.d, ss.d])
        S_.op("dve", lambda e: e.tensor_scalar(out=ms.ap, in0=ss.ap, scalar1=1.0 / D, scalar2=EPS, op0=ALU.mult, op1=ALU.add),
              reads=[ss.d], writes=[ms.d])
        S_.op("act", lambda e: e.sqrt(out=sd.ap, in_=ms.ap), reads=[ms.d], writes=[sd.d])
        S_.op("dve", lambda e: e.reciprocal(out=rstd.ap, in_=sd.ap), reads=[sd.d], writes=[rstd.d])
        S_.op("act", lambda e: e.activation(out=u.ap, in_=xt.ap, func=AF.Copy, scale=rstd.ap),
              reads=[xt.d, rstd.d], writes=[u.d])
        for c4 in range(0, c.DC, 4):
            ps = self.ps_next()
            for j in range(4):
                cc = c4 + j
                S_.op("pe", lambda e, cc=cc, j=j, ps=ps: e.transpose(out=ps.ap[:, j * 128:(j + 1) * 128],
                                                                      in_=u.ap[:, cc * 128:(cc + 1) * 128], identity=self.ident.ap),
                      reads=[u.d, self.ident.d], writes=[ps.d])
            for j in range(4):
                cc = c4 + j
                if j % 2 == 0:
                    S_.op("dve", lambda e, cc=cc, j=j, ps=ps: e.tensor_scalar(out=uT3.ap[:, cc, col0:col0 + 128], in0=ps.ap[:, j * 128:(j + 1) * 128],
                                                                             scalar1=gam.ap[:, cc:cc + 1], scalar2=None, op0=ALU.mult),
                          reads=[ps.d, gam.d], writes=[uT3.d])
                else:
                    S_.op("act", lambda e, cc=cc, j=j, ps=ps: e.activation(out=uT3.ap[:, cc, col0:col0 + 128], in_=ps.ap[:, j * 128:(j + 1) * 128],
                                                                          func=AF.Copy, scale=gam.ap[:, cc:cc + 1]),
                          reads=[ps.d, gam.d], writes=[uT3.d])

    def smalls(self):
        return tuple(self.tf32(1) for _ in range(4))

    def mm_fm(self, ps, wv, wd, j0, jw, act3, actd, n0, n, kc, prow=None):
        for cc in range(kc):
            self.S.op("pe", lambda e, cc=cc: e.matmul(ps.ap[0:jw, 0:n], lhsT=wv[:, cc, j0:j0 + jw], rhs=act3[:, cc, n0:n0 + n],
                                                      start=(cc == 0), stop=(cc == kc - 1)),
                      reads=list(wd) + [actd], writes=[ps.d])

    def mm_tm(self, ps, wv, wd, j0, jw, act3, actd, t0, kc):
        for cc in range(kc):
            self.S.op("pe", lambda e, cc=cc: e.matmul(ps.ap[:, 0:jw], lhsT=act3[:, cc, t0:t0 + 128], rhs=wv[:, cc, j0:j0 + jw],
                                                      start=(cc == 0), stop=(cc == kc - 1)),
                      reads=list(wd) + [actd], writes=[ps.d])

    def phase_inproj(self):
        c = self.cfg
        S_ = self.S
        D, G, H, DC = c.D, c.G, c.H, c.DC
        NG = c.S // G
        OWNG = c.TOWN // G
        TPG = G // 128
        self.reset_arena()
        xts = [self.tf32(D) for _ in range(2)]
        junk = self.tbf(D)
        uT = self.tbf(DC * G)
        uT3 = Tl(uT.ap.rearrange("p (c t) -> p c t", c=DC), uT.d)
        wbufs = [WBuf(self.abf(DC * 512)) for _ in range(2)]
        smalls2 = [self.smalls() for _ in range(2)]
        xi = 0
        stg = [self.tbf(4 * G) for _ in range(3)]
        ki_st = Tl(self.af32((G + 1) // 2).bitcast(FP16)[:, 0:G])
        wi_st = self.tf32(TPG * c.IH)
        wi_st3 = wi_st.ap.rearrange("p (t n) -> p t n", t=TPG)
        glu_st = [self.tf32(2 * G) for _ in range(2)]
        sg_tmp = [self.tf32(G) for _ in range(2)]
        gate_st = [self.tf32(G) for _ in range(2)]
        wi_scale = (c.IH ** -0.5) * (64 ** -0.5)
        pi = [0]
        si = [0]

        def next_w():
            b = wbufs[pi[0] % 2]
            pi[0] += 1
            return b

        def next_s():
            b = stg[si[0] % 3]
            si[0] += 1
            return b

        def heads_fm(off, nunits, dst, evac_scale, evac_eng, slot0, f16=False):
            for i in range(0, nunits, 4):
                nh = min(4, nunits - i)
                wb = next_w()
                wv = self.load_panel(None, "in", wb, [(off + i * 128, nh * 128)], DC)
                st = next_s()
                st3 = (st.ap.bitcast(FP16) if f16 else st.ap).rearrange("p (h t) -> p h t", h=4)
                for j in range(nh):
                    ps = self.ps_next()
                    self.mm_fm(ps, wv, wb.used, j * 128, 128, uT3.ap, uT3.d, 0, G, DC)
                    if evac_eng == "act":
                        S_.op("act", lambda e, ps=ps, j=j, st3=st3: e.activation(out=st3[:, j, :], in_=ps.ap[:, 0:G], func=AF.Copy, scale=evac_scale),
                              reads=[ps.d], writes=[st.d])
                    else:
                        S_.op("dve", lambda e, ps=ps, j=j, st3=st3: e.tensor_copy(out=st3[:, j, :], in_=ps.ap[:, 0:G]),
                              reads=[ps.d], writes=[st.d])
                self.dma("pool", dst[i:i + nh, :, slot0:slot0 + G].rearrange("h p s -> p h s"), st3[:, 0:nh, :], reads=[st.d])

        order = [g for g in range(OWNG + 1, NG)] + [OWNG] + list(range(OWNG))
        n_rest = len(self.plist["in"]) * ((DC + 7) // 8) * 2
        per_g = (n_rest + max(1, NG - OWNG - 1) - 1) // max(1, NG - OWNG - 1)
        for g in order:
            own = g < OWNG
            halo = g == OWNG
            slot0 = g * G
            if not own:
                self.pump_casts(per_g)
            for t in range(TPG):
                xt = xts[xi % 2]
                small = smalls2[xi % 2]
                xi += 1
                self.dma("sp", xt.ap, self.xk[slot0 + t * 128: slot0 + (t + 1) * 128, :], writes=[xt.d])
                self.norm_transpose(xt, xt, junk, uT3, t * 128, self.g1, small)
            heads_fm(c.OFF_K, H, self.KT, 1.0, "act", slot0)
            for i in range(0, c.HW, 512):
                w = min(512, c.HW - i)
                wb = next_w()
                wv = self.load_panel(None, "in", wb, [(c.OFF_V + i, w)], DC)
                st = next_s()
                st3 = st.ap[:, 0:TPG * w].rearrange("p (t n) -> p t n", t=TPG)
                for t in range(TPG):
                    ps = self.ps_next()
                    self.mm_tm(ps, wv, wb.used, 0, w, uT3.ap, uT3.d, t * 128, DC)
                    S_.op("dve", lambda e, ps=ps, t=t, w=w, st3=st3: e.tensor_copy(out=st3[:, t, :], in_=ps.ap[:, 0:w]),
                          reads=[ps.d], writes=[st.d])
                for hh in range(w // 128):
                    self.dma("pool", self.V[i // 128 + hh, :, slot0:slot0 + G].rearrange("p (t d) -> p t d", t=TPG),
                             st3[:, :, hh * 128:(hh + 1) * 128], reads=[st.d])
            wb = next_w()
            wv = self.load_panel(None, "in", wb, [(c.OFF_IK, 64), (c.OFF_IK, 64), (c.OFF_IW, c.IH)], DC)
            ps = self.ps_next()
            self.mm_fm(ps, wv, wb.used, 0, 128, uT3.ap, uT3.d, 0, G, DC)
            S_.op("act", lambda e, ps=ps: e.copy(out=ki_st.ap, in_=ps.ap[:, 0:G]), reads=[ps.d], writes=[ki_st.d])
            self.dma("pool", self.kiT[:, slot0:slot0 + G], ki_st.ap, reads=[ki_st.d])
            if own:
                for t in range(TPG):
                    ps = self.ps_next()
                    self.mm_tm(ps, wv, wb.used, 128, c.IH, uT3.ap, uT3.d, t * 128, DC)
                    S_.op("act", lambda e, ps=ps, t=t: e.activation(out=wi_st3[:, t, :], in_=ps.ap[:, 0:c.IH], func=AF.Copy, scale=wi_scale),
                          reads=[ps.d], writes=[wi_st.d])
                self.dma("pool", self.wi[slot0:slot0 + G, :].rearrange("(t p) n -> p t n", p=128), wi_st3, reads=[wi_st.d])
            if own or halo:
                n0, n = (0, G) if own else (96, 32)
                dst0 = 32 + slot0 if own else 0
                for i in range(0, c.CC, 2):
                    ncc = min(2, c.CC - i)
                    wb = next_w()
                    wv = self.load_panel(None, "in", wb, [(c.OFF_A + i * 128, ncc * 128), (c.OFF_G + i * 128, ncc * 128)], DC)
                    gs = glu_st[(i // 2) % 2]
                    for j in range(ncc):
                        psa = self.ps_next()
                        self.mm_fm(psa, wv, wb.used, j * 128, 128, uT3.ap, uT3.d, n0, n, DC)
                        psg = self.ps_next()
                        self.mm_fm(psg, wv, wb.used, (ncc + j) * 128, 128, uT3.ap, uT3.d, n0, n, DC)
                        sg = sg_tmp[j % 2]
                        S_.op("act", lambda e, psg=psg, sg=sg, n=n: e.activation(out=sg.ap[:, 0:n], in_=psg.ap[:, 0:n], func=AF.Sigmoid),
                              reads=[psg.d], writes=[sg.d])
                        S_.op("dve", lambda e, psa=psa, sg=sg, gs=gs, j=j, n=n: e.tensor_tensor(out=gs.ap[:, j * G:j * G + n], in0=sg.ap[:, 0:n],
                                                                                      in1=psa.ap[:, 0:n], op=ALU.mult),
                              reads=[psa.d, sg.d], writes=[gs.d])
                    gv = gs.ap[:, 0:ncc * G].rearrange("p (c t) -> p c t", c=ncc)
                    self.dma("pool", self.gluT[i:i + ncc, :, dst0:dst0 + n].rearrange("c p s -> p c s"), gv[:, :, 0:n], reads=[gs.d])
            if own:
                heads_fm(c.OFF_Q, H, self.QT, 128 ** -0.5, "act", slot0)
                heads_fm(c.OFF_IQ, c.IH // 2, self.qiT, 1.0, "dve", slot0, f16=True)
                for i in range(0, 2 * DC, 4):
                    wb = next_w()
                    wv = self.load_panel(None, "in", wb, [(c.OFF_GATE + i * 128, 512)], DC)
                    for j in range(4):
                        f = i + j
                        gt = gate_st[f % 2]
                        ps = self.ps_next()
                        self.mm_fm(ps, wv, wb.used, j * 128, 128, uT3.ap, uT3.d, 0, G, DC)
                        S_.op("act", lambda e, ps=ps, f=f, gt=gt: e.activation(out=gt.ap, in_=ps.ap[:, 0:G],
                                                                               func=AF.Sigmoid, bias=self.bg.ap[:, f:f + 1]),
                              reads=[ps.d, self.bg.d], writes=[gt.d])
                        self.dma("pool", self.gatesT[f, :, slot0:slot0 + G], gt.ap, reads=[gt.d])

    def phase_conv(self):
        c = self.cfg
        S_ = self.S
        G, CC = c.G, c.CC
        self.reset_arena()
        glu = self.tf32(CC * (32 + G))
        glu3 = glu.ap.rearrange("p (c t) -> p c t", c=CC)
        y = self.tf32(CC * G)
        y3 = y.ap.rearrange("p (c t) -> p c t", c=CC)
        ysq = self.tf32(CC * G)
        ysq3 = ysq.ap.rearrange("p (c t) -> p c t", c=CC)
        mean, msq, var, rstd = (self.tf32(G) for _ in range(4))
        t1 = [self.tf32(G) for _ in range(2)]
        z_st = self.tbf(CC * G)
        z3 = z_st.ap.rearrange("p (c t) -> p c t", c=CC)
        cw3 = self.cw.ap.rearrange("p (c k) -> p c k", c=CC)
        for g in range(c.TOWN // G):
            tok0 = g * G
            self.dma("sp", glu3, self.gluT[:, :, tok0:tok0 + 32 + G].rearrange("c p s -> p c s"), writes=[glu.d])
            for cc in range(CC):
                S_.op("dve", lambda e, cc=cc: e.tensor_scalar(out=y3[:, cc, :], in0=glu3[:, cc, 2:2 + G], scalar1=cw3[:, cc, 0:1],
                                                            scalar2=self.cb.ap[:, cc:cc + 1], op0=ALU.mult, op1=ALU.add),
                      reads=[glu.d, self.cw.d, self.cb.d], writes=[y.d])
                for k in range(1, 31):
                    S_.op("dve", lambda e, cc=cc, k=k: e.scalar_tensor_tensor(out=y3[:, cc, :], in0=glu3[:, cc, 2 + k:2 + k + G],
                                                                             scalar=cw3[:, cc, k:k + 1], in1=y3[:, cc, :],
                                                                             op0=ALU.mult, op1=ALU.add),
                          reads=[glu.d, self.cw.d], writes=[y.d])
            S_.op("act", lambda e: e.activation(out=ysq.ap, in_=y.ap, func=AF.Square), reads=[y.d], writes=[ysq.d])
            ps_s = self.ps_next()
            for cc in range(CC):
                S_.op("pe", lambda e, cc=cc: e.matmul(ps_s.ap[:, 0:G], lhsT=self.ones32.ap, rhs=y3[:, cc, :], start=(cc == 0), stop=(cc == CC - 1)),
                      reads=[y.d, self.ones32.d], writes=[ps_s.d])
            ps_q = self.ps_next()
            for cc in range(CC):
                S_.op("pe", lambda e, cc=cc: e.matmul(ps_q.ap[:, 0:G], lhsT=self.ones32.ap, rhs=ysq3[:, cc, :], start=(cc == 0), stop=(cc == CC - 1)),
                      reads=[ysq.d, self.ones32.d], writes=[ps_q.d])
            S_.op("act", lambda e: e.activation(out=mean.ap, in_=ps_s.ap[:, 0:G], func=AF.Copy, scale=1.0 / c.CH), reads=[ps_s.d], writes=[mean.d])
            S_.op("dve", lambda e: e.tensor_tensor(out=msq.ap, in0=mean.ap, in1=mean.ap, op=ALU.mult), reads=[mean.d], writes=[msq.d])
            S_.op("dve", lambda e: e.scalar_tensor_tensor(out=var.ap, in0=ps_q.ap[:, 0:G], scalar=1.0 / c.CH, in1=msq.ap, op0=ALU.mult, op1=ALU.subtract),
                  reads=[ps_q.d, msq.d], writes=[var.d])
            S_.op("dve", lambda e: e.tensor_scalar(out=var.ap, in0=var.ap, scalar1=EPS, scalar2=None, op0=ALU.add), reads=[var.d], writes=[var.d])
            S_.op("act", lambda e: e.sqrt(out=msq.ap, in_=var.ap), reads=[var.d], writes=[msq.d])
            S_.op("dve", lambda e: e.reciprocal(out=rstd.ap, in_=msq.ap), reads=[msq.d], writes=[rstd.d])
            for cc in range(CC):
                tt = t1[cc % 2]
                S_.op("dve", lambda e, cc=cc, tt=tt: e.tensor_tensor(out=tt.ap, in0=y3[:, cc, :], in1=mean.ap, op=ALU.subtract),
                      reads=[y.d, mean.d], writes=[tt.d])
                S_.op("dve", lambda e, tt=tt: e.tensor_tensor(out=tt.ap, in0=tt.ap, in1=rstd.ap, op=ALU.mult), reads=[tt.d, rstd.d], writes=[tt.d])
                S_.op("act", lambda e, cc=cc, tt=tt: e.activation(out=z3[:, cc, :], in_=tt.ap, func=AF.Silu, bias=self.lb.ap[:, cc:cc + 1],
                                                                scale=self.lg.ap[:, cc:cc + 1]),
                      reads=[tt.d, self.lg.d, self.lb.d], writes=[z_st.d])
            self.dma("pool", self.zT[:, :, tok0:tok0 + G].rearrange("c p s -> p c s"), z3, reads=[z_st.d])

    def phase_indexer(self):
        c = self.cfg
        S_ = self.S
        S, IH, NSC = c.S, c.IH, c.NSC
        NHP = IH // 2
        NT = c.TOWN // 128
        LOOK = 3
        self.reset_arena()
        kiT = Tl(self.af32(S // 2).bitcast(FP16))
        kpos = Tl(self.af32(S // 2).bitcast(I16))
        penb = self.tbf(S)
        identbf = self.tbf(128)
        accs = [self.tf32(S) for _ in range(2)]
        mk = self.tf32(S)
        identb = Tl(self.af32(64).bitcast(FP16))
        rb = [Tl(self.af32(256).bitcast(FP16)) for _ in range(LOOK + 2)]
        qpad = Tl(self.af32(IH * 64).bitcast(FP16))
        qpad4 = qpad.ap.rearrange("p (h two q) -> p h two q", h=NHP, two=2)
        qpad3 = qpad.ap.rearrange("p (h q) -> p h q", h=IH)
        wi_t = [self.tf32(IH) for _ in range(2)]
        dgw = [Tl(self.af32(IH * 64).bitcast(FP16)) for _ in range(1)]
        absw = [self.tf32(IH) for _ in range(2)]
        sgnw = [self.tf32(IH) for _ in range(2)]
        MH = max(4, NSC // 2)
        mT_st = self.tbf(MH * 128)
        mT3 = mT_st.ap.rearrange("p (c q) -> p c q", c=MH)
        lo, hi, mid, cnt, ge, dd, nm = (self.tf32(1) for _ in range(7))
        pss = [self.ps[0], self.ps[1], self.ps[2], self.ps[3]]
        pacc = [self.ps[4], self.ps[5]]
        ptr = [self.ps[6], self.ps[7]]
        self.dma("sp", kiT.ap, self.kiT, writes=[kiT.d])
        self.dma("sp", kpos.ap, self.kpos_in, writes=[kpos.d])
        S_.op("dve", lambda e: e.tensor_copy(out=identb.ap, in_=self.ident.ap), reads=[self.ident.d], writes=[identb.d])
        S_.op("dve", lambda e: e.memset(qpad.ap, 0.0), writes=[qpad.d])
        S_.op("dve", lambda e: e.tensor_copy(out=identbf.ap, in_=self.ident.ap), reads=[self.ident.d], writes=[identbf.d])

        def unit_stream(t):
            acc = accs[t % 2]
            wt = wi_t[t % 2]
            dg = dgw[0]
            dg3 = dg.ap.rearrange("p (h q) -> p h q", h=IH)
            self.dma("sp", qpad4[0:64, :, 0, :], self.qiT[:, 0:64, t * 128:(t + 1) * 128].rearrange("h p q -> p h q"), writes=[qpad.d])
            qd2 = Dep()
            self.dma("sp", qpad4[64:128, :, 1, :], self.qiT[:, 64:128, t * 128:(t + 1) * 128].rearrange("h p q -> p h q"), reads=[qpad.d], writes=[qd2])
            self.dma("sp", wt.ap, self.wi[t * 128:(t + 1) * 128, :], writes=[wt.d])
            self.pump_casts(5)
            qp = self.qpos.ap[:, t:t + 1]
            aw = absw[t % 2]
            sg = sgnw[t % 2]
            S_.op("dve", lambda e, wt=wt, sg=sg: e.tensor_scalar(out=sg.ap, in0=wt.ap, scalar1=0.0, scalar2=2.0, op0=ALU.is_ge, op1=ALU.mult), reads=[wt.d], writes=[sg.d])
            S_.op("dve", lambda e, sg=sg: e.tensor_scalar(out=sg.ap, in0=sg.ap, scalar1=-1.0, scalar2=None, op0=ALU.add), reads=[sg.d], writes=[sg.d])
            S_.op("dve", lambda e, wt=wt, aw=aw, sg=sg: e.tensor_tensor(out=aw.ap, in0=wt.ap, in1=sg.ap, op=ALU.mult), reads=[wt.d, sg.d], writes=[aw.d])
            for h in range(IH):
                S_.op("dve", lambda e, h=h, sg=sg, dg3=dg3: e.tensor_scalar(out=dg3[:, h, :], in0=identb.ap, scalar1=sg.ap[:, h:h + 1], scalar2=None,
                                                                             op0=ALU.mult),
                      reads=[identb.d, sg.d], writes=[dg.d])
            S_.op("dve", lambda e, qp=qp: e.tensor_scalar(out=penb.ap, in0=kpos.ap, scalar1=qp, scalar2=NEG, op0=ALU.is_gt, op1=ALU.mult),
                  reads=[kpos.d, self.qpos.d], writes=[penb.d])
            yield
            units = [(s5i, h) for s5i in range(S // 512) for h in range(IH)]

            def A(u):
                s5i, h = units[u]
                hp, half = divmod(h, 2)
                r0 = half * 64
                ps = pss[u % 4]
                r = rb[u % len(rb)]
                S_.op("pe", lambda e, ps=ps, h=h, s5i=s5i: e.matmul(ps.ap[:, 0:512], lhsT=qpad3[:, h, :],
                                                                  rhs=kiT.ap[:, s5i * 512:(s5i + 1) * 512], start=True, stop=True),
                      reads=[qpad.d, qd2, kiT.d], writes=[ps.d])
                S_.op("act", lambda e, ps=ps, r=r, h=h: e.activation(out=r.ap, in_=ps.ap[:, 0:512], func=AF.Relu, scale=aw.ap[:, h:h + 1]),
                      reads=[ps.d, aw.d], writes=[r.d])

            def B(u):
                s5i, h = units[u]
                r = rb[u % len(rb)]
                pa = pacc[s5i % 2]
                if h == 0:
                    S_.op("pe", lambda e, pa=pa, s5i=s5i: e.matmul(pa.ap[:, 0:512], lhsT=identbf.ap, rhs=penb.ap[:, s5i * 512:(s5i + 1) * 512], start=True, stop=False),
                          reads=[penb.d, identbf.d], writes=[pa.d])
                S_.op("pe", lambda e, r=r, h=h, pa=pa: e.matmul(pa.ap[:, 0:512], lhsT=dg3[:, h, :], rhs=r.ap, start=False, stop=(h == IH - 1)),
                      reads=[r.d, dg.d], writes=[pa.d])
                if h == IH - 1:
                    S_.op("act", lambda e, pa=pa, s5i=s5i, acc=acc: e.copy(out=acc.ap[:, s5i * 512:(s5i + 1) * 512], in_=pa.ap[:, 0:512]),
                          reads=[pa.d], writes=[acc.d])

            n = len(units)
            for u in range(n + LOOK):
                if u < n:
                    A(u)
                if u - LOOK >= 0:
                    B(u - LOOK)
                yield

        def post_stream(t):
            acc = accs[t % 2]
            qp = self.qpos.ap[:, t:t + 1]
            S_.op("dve", lambda e, qp=qp: e.tensor_scalar(out=mk.ap, in0=kpos.ap, scalar1=qp, scalar2=-1.0, op0=ALU.is_le, op1=ALU.mult),
                  reads=[kpos.d, self.qpos.d], writes=[mk.d])
            yield
            S_.op("dve", lambda e, acc=acc: e.tensor_tensor(out=mk.ap, in0=mk.ap, in1=acc.ap, op=ALU.mult), reads=[acc.d], writes=[mk.d])
            S_.op("dve", lambda e: e.reduce_max(out=nm.ap, in_=mk.ap, axis=AX.X), reads=[mk.d], writes=[nm.d])
            yield
            S_.op("dve", lambda e: e.tensor_scalar(out=lo.ap, in0=nm.ap, scalar1=-1.0, scalar2=-1e-3, op0=ALU.mult, op1=ALU.add),
                  reads=[nm.d], writes=[lo.d])
            S_.op("dve", lambda e, acc=acc: e.reduce_max(out=hi.ap, in_=acc.ap, axis=AX.X), reads=[acc.d], writes=[hi.d])
            yield
            for it in range(NITER):
                S_.op("dve", lambda e: e.tensor_tensor(out=mid.ap, in0=lo.ap, in1=hi.ap, op=ALU.add), reads=[lo.d, hi.d], writes=[mid.d])
                S_.op("dve", lambda e: e.tensor_scalar(out=mid.ap, in0=mid.ap, scalar1=0.5, scalar2=None, op0=ALU.mult), reads=[mid.d], writes=[mid.d])
                S_.op("dve", lambda e, acc=acc: e.tensor_scalar(out=mk.ap, in0=acc.ap, scalar1=mid.ap, scalar2=None, op0=ALU.is_ge, op1=ALU.add, accum_out=cnt.ap),
                      reads=[acc.d, mid.d], writes=[mk.d, cnt.d])
                S_.op("dve", lambda e: e.tensor_single_scalar(out=ge.ap, in_=cnt.ap, scalar=c.TOPK - 0.5, op=ALU.is_ge), reads=[cnt.d], writes=[ge.d])
                S_.op("dve", lambda e: e.tensor_tensor(out=dd.ap, in0=mid.ap, in1=lo.ap, op=ALU.subtract), reads=[mid.d, lo.d], writes=[dd.d])
                S_.op("dve", lambda e: e.scalar_tensor_tensor(out=lo.ap, in0=dd.ap, scalar=ge.ap, in1=lo.ap, op0=ALU.mult, op1=ALU.add),
                      reads=[dd.d, ge.d], writes=[lo.d])
                S_.op("dve", lambda e: e.tensor_tensor(out=dd.ap, in0=hi.ap, in1=mid.ap, op=ALU.subtract), reads=[mid.d, hi.d], writes=[dd.d])
                S_.op("dve", lambda e: e.scalar_tensor_tensor(out=hi.ap, in0=dd.ap, scalar=ge.ap, in1=mid.ap, op0=ALU.mult, op1=ALU.add),
                      reads=[dd.d, ge.d, mid.d], writes=[hi.d])
                yield
            S_.op("dve", lambda e, acc=acc: e.tensor_scalar(out=mk.ap, in0=acc.ap, scalar1=lo.ap, scalar2=None, op0=ALU.is_ge), reads=[acc.d, lo.d], writes=[mk.d])
            yield "HOLD"
            for c4 in range(0, NSC, 4):
                ps = ptr[(c4 // 4) % 2]
                for j in range(4):
                    sc = c4 + j
                    S_.op("pe", lambda e, ps=ps, j=j, sc=sc: e.transpose(out=ps.ap[:, j * 128:(j + 1) * 128], in_=mk.ap[:, sc * 128:(sc + 1) * 128],
                                                                        identity=self.ident.ap),
                          reads=[mk.d, self.ident.d], writes=[ps.d])
                cl = c4 % MH
                S_.op("act", lambda e, ps=ps, cl=cl: e.copy(out=mT3[:, cl:cl + 4, :], in_=ps.ap.rearrange("p (c q) -> p c q", c=4)),
                      reads=[ps.d], writes=[mT_st.d])
                if cl + 4 == MH:
                    self.dma("pool", self.maskT[c4 + 4 - MH:c4 + 4, :, t * 128:(t + 1) * 128].rearrange("c p q -> p c q"), mT3, reads=[mT_st.d])
                yield

        for t in range(NT + 1):
            us = unit_stream(t) if t < NT else None
            pst = post_stream(t - 1) if t >= 1 else None
            nu = (S // 512) * IH + LOOK + 1
            npst = NITER + 5
            ratio = max(1, int(nu * 0.85) // (npst + 2))
            hold = False
            while us is not None or pst is not None:
                if us is not None:
                    for _ in range(ratio):
                        try:
                            next(us)
                        except StopIteration:
                            us = None
                            break
                if pst is not None and not (hold and us is not None):
                    try:
                        if next(pst) == "HOLD":
                            hold = True
                    except StopIteration:
                        pst = None

    def phase_attn(self):
        c = self.cfg
        S_ = self.S
        S, H, NSC, QB = c.S, c.H, c.NSC, c.QB
        QTP = QB // 128
        NPO = (QTP + 1) // 2
        LOOK = 3
        self.reset_arena()
        mqb = self.tbf(NSC * QB)
        mqb3 = mqb.ap.rearrange("p (c q) -> p c q", c=NSC)
        etab = self.tf32(H * 2 * 128)
        etab4 = etab.ap.rearrange("p (h a q) -> p h a q", h=H, a=2)
        KTh = [self.tbf(S) for _ in range(2)]
        QTh = [self.tbf(QB) for _ in range(2)]
        V1 = [self.tbf(NSC * 130) for _ in range(2)]
        pT = [self.tbf(QB) for _ in range(LOOK + 2)]
        o_sb = [self.tf32(128) for _ in range(2)]
        rden = [self.tf32(1) for _ in range(2)]
        oT_st = [self.tbf(QB) for _ in range(2)]
        nrb = self.tf32(H)
        self.dma("sp", etab.ap, self.ttab_in, writes=[etab.d])
        S_.op("dve", lambda e: e.tensor_scalar(out=nrb.ap, in0=self.rb31.ap, scalar1=-1.0, scalar2=None, op0=ALU.mult), reads=[self.rb31.d], writes=[nrb.d])
        for h in range(H):
            S_.op("act", lambda e, h=h: e.activation(out=etab4[:, h, :, :], in_=etab4[:, h, :, :], func=AF.Exp, bias=nrb.ap[:, h:h + 1]),
                  reads=[etab.d, nrb.d], writes=[etab.d])
        for v in V1:
            S_.op("dve", lambda e, v=v: e.memset(v.ap, 1.0), writes=[v.d])
        if NPO == 2:
            pos = [[self.ps[4], self.ps[5]], [self.ps[6], self.ps[7]]]
            pss = [self.ps[0], self.ps[1], self.ps[2]]
            pst = [self.ps[3]]
        else:
            pos = [[self.ps[4]], [self.ps[5]]]
            pss = [self.ps[0], self.ps[1], self.ps[2], self.ps[3]]
            pst = [self.ps[6], self.ps[7]]
        items = [(qb, h, sc) for qb in range(c.TOWN // QB) for h in range(H) for sc in range(NSC)]
        st = {}

        def stageA(i):
            qb, h, sc = items[i]
            hh = qb * H + h
            q0 = qb * QB
            kt, qt, v1 = KTh[hh % 2], QTh[hh % 2], V1[hh % 2]
            po = pos[hh % 2]
            v13 = v1.ap.rearrange("p (c d) -> p c d", c=NSC)
            if sc == 0:
                if h == 0:
                    self.dma("sp", mqb3, self.maskT[:, :, q0:q0 + QB].rearrange("c p q -> p c q"), writes=[mqb.d])
                self.pump_casts(7)
                self.dma("sp", kt.ap, self.KT[h, :, :], writes=[kt.d])
                self.dma("sp", qt.ap, self.QT[h, :, q0:q0 + QB], writes=[qt.d])
                self.dma("sp", v13[:, :, 0:128], self.V[h, :, :].rearrange("p (c d) -> p c d", c=NSC), writes=[v1.d])
                for b in range(NPO):
                    S_.op("pe", lambda e, b=b, po=po: e.matmul(po[b].ap[:, 0:512], lhsT=self.zbf.ap[:, 0:128], rhs=self.zbf.ap[:, 0:512], start=True, stop=False,
                                                               skip_group_check=True),
                          reads=[self.zbf.d], writes=[po[b].d])
            ps = pss[i % len(pss)]
            S_.op("pe", lambda e, ps=ps, sc=sc, kt=kt, qt=qt: e.matmul(ps.ap[:, 0:QB], lhsT=kt.ap[:, sc * 128:(sc + 1) * 128], rhs=qt.ap,
                                                                      start=True, stop=True),
                  reads=[kt.d, qt.d], writes=[ps.d])
            p = pT[i % len(pT)]
            S_.op("act", lambda e, ps=ps, p=p, h=h: e.activation(out=p.ap, in_=ps.ap[:, 0:QB], func=AF.Exp, bias=self.rb31.ap[:, h:h + 1]),
                  reads=[ps.d, self.rb31.d], writes=[p.d])
            S_.op("dve", lambda e, p=p, sc=sc: e.tensor_tensor(out=p.ap, in0=p.ap, in1=mqb3[:, sc, :], op=ALU.mult),
                  reads=[mqb.d], writes=[p.d])
            for tq in range(QTP):
                t = qb * QTP + tq
                pc = t - 1 if t >= 1 else c.HALO_CHUNK
                for a, chunk in ((0, t), (1, pc)):
                    if sc == chunk:
                        S_.op("pool", lambda e, p=p, tq=tq, a=a, h=h: e.tensor_tensor(out=p.ap[:, tq * 128:(tq + 1) * 128],
                                                                                     in0=p.ap[:, tq * 128:(tq + 1) * 128],
                                                                                     in1=etab4[:, h, a, :], op=ALU.mult),
                              reads=[etab.d], writes=[p.d])

        def stageB(i):
            qb, h, sc = items[i]
            hh = qb * H + h
            q0 = qb * QB
            v1 = V1[hh % 2]
            po = pos[hh % 2]
            ost = oT_st[hh % 2]
            v13 = v1.ap.rearrange("p (c d) -> p c d", c=NSC)
            p = pT[i % len(pT)]
            for tq in range(QTP):
                b, o = divmod(tq, 2)
                S_.op("pe", lambda e, p=p, tq=tq, b=b, o=o, sc=sc, v13=v13, po=po: e.matmul(po[b].ap[:, o * 129:o * 129 + 129],
                                                                                         lhsT=p.ap[:, tq * 128:(tq + 1) * 128], rhs=v13[:, sc, 0:129],
                                                                                         start=False, stop=(sc == NSC - 1), skip_group_check=True),
                      reads=[p.d, v1.d], writes=[po[b].d])
            if sc == NSC - 1:
                pt = pst[hh % len(pst)]
                for tq in range(QTP):
                    b, o = divmod(tq, 2)
                    rd = rden[tq % 2]
                    osb = o_sb[tq % 2]
                    S_.op("dve", lambda e, b=b, o=o, rd=rd, po=po: e.reciprocal(out=rd.ap, in_=po[b].ap[:, o * 129 + 128:o * 129 + 129]), reads=[po[b].d], writes=[rd.d])
                    S_.op("dve", lambda e, b=b, o=o, rd=rd, osb=osb, po=po: e.tensor_scalar(out=osb.ap, in0=po[b].ap[:, o * 129:o * 129 + 128], scalar1=rd.ap,
                                                                                          scalar2=None, op0=ALU.mult),
                          reads=[po[b].d, rd.d], writes=[osb.d])
                    S_.op("pe", lambda e, tq=tq, osb=osb, pt=pt: e.transpose(out=pt.ap[:, tq * 128:(tq + 1) * 128], in_=osb.ap, identity=self.ident.ap),
                          reads=[osb.d, self.ident.d], writes=[pt.d])
                S_.op("act", lambda e, pt=pt, ost=ost: e.copy(out=ost.ap, in_=pt.ap[:, 0:QB]), reads=[pt.d], writes=[ost.d])
                self.dma("pool", self.oT[h, :, q0:q0 + QB], ost.ap, reads=[ost.d])

        n = len(items)
        for i in range(n + LOOK):
            if i < n:
                stageA(i)
            if i - LOOK >= 0:
                stageB(i - LOOK)

    def phase_mix(self):
        c = self.cfg
        S_ = self.S
        D, G, DC, CC, H = c.D, c.G, c.DC, c.CC, c.H
        TPG = G // 128
        self.reset_arena()
        zt = self.tbf(CC * G)
        zt3 = zt.ap.rearrange("p (c t) -> p c t", c=CC)
        ot = self.tbf(H * G)
        ot3 = ot.ap.rearrange("p (c t) -> p c t", c=H)
        mixT = self.tbf(DC * G)
        mix3 = mixT.ap.rearrange("p (c t) -> p c t", c=DC)
        wbufs = [WBuf(self.abf(max(DC, CC + H) * 512)) for _ in range(2)]
        wi_ = 0
        ga = [self.tf32(G) for _ in range(2)]
        gb = [self.tf32(G) for _ in range(2)]
        m1 = [self.tf32(G) for _ in range(2)]
        m2 = [self.tf32(G) for _ in range(2)]
        xp = [self.tf32(512) for _ in range(2)]
        hst = [self.tf32(512) for _ in range(2)]
        k = 0
        for g in range(c.TOWN // G):
            tok0 = g * G
            self.dma("sp", zt3, self.zT[:, :, tok0:tok0 + G].rearrange("c p s -> p c s"), writes=[zt.d])
            self.dma("sp", ot3, self.oT[:, :, tok0:tok0 + G].rearrange("c p s -> p c s"), writes=[ot.d])
            for n in range(D // 512):
                wb = wbufs[wi_ % 2]
                wi_ += 1
                prev = list(wb.used)
                wb.used = [wb.ds[0], wb.ds[1]]
                wav = wb.ap[:, 0:CC * 512].rearrange("p (c n) -> p c n", c=CC)
                wbv = wb.ap[:, CC * 512:(CC + H) * 512].rearrange("p (c n) -> p c n", c=H)
                ia = self.preg[("co", ((n * 512, 512),), CC, 0)]
                ib = self.preg[("ao", ((n * 512, 512),), H, 0)]
                self.dma("sp", wb.ap[:, 0:CC * 512], self.wbp["co"][ia, :, 0:CC * 512], reads=self.pdeps[("co", ia)],
                         writes=[wb.ds[0]] + [d for d in prev if d is not wb.ds[0]])
                self.dma("sp", wb.ap[:, CC * 512:(CC + H) * 512], self.wbp["ao"][ib, :, 0:H * 512], reads=self.pdeps[("ao", ib)],
                         writes=[wb.ds[1]] + [d for d in prev if d is not wb.ds[1]])
                for j in range(4):
                    f = n * 4 + j
                    psa = self.ps_next()
                    self.mm_fm(psa, wav, [wb.ds[0]], j * 128, 128, zt3, zt.d, 0, G, CC)
                    psb = self.ps_next()
                    self.mm_fm(psb, wbv, [wb.ds[1]], j * 128, 128, ot3, ot.d, 0, G, H)
                    gA, gB, mA, mB = ga[k % 2], gb[k % 2], m1[k % 2], m2[k % 2]
                    k += 1
                    self.dma("sp", gA.ap, self.gatesT[f, :, tok0:tok0 + G], writes=[gA.d])
                    self.dma("sp", gB.ap, self.gatesT[DC + f, :, tok0:tok0 + G], writes=[gB.d])
                    S_.op("dve", lambda e, psa=psa, gA=gA, mA=mA: e.tensor_tensor(out=mA.ap, in0=gA.ap, in1=psa.ap[:, 0:G], op=ALU.mult),
                          reads=[psa.d, gA.d], writes=[mA.d])
                    S_.op("dve", lambda e, psb=psb, gB=gB, mB=mB: e.tensor_tensor(out=mB.ap, in0=gB.ap, in1=psb.ap[:, 0:G], op=ALU.mult),
                          reads=[psb.d, gB.d], writes=[mB.d])
                    S_.op("pool", lambda e, mA=mA, mB=mB, f=f: e.tensor_tensor(out=mix3[:, f, :], in0=mA.ap, in1=mB.ap, op=ALU.add),
                          reads=[mA.d, mB.d], writes=[mixT.d])
            for n in range(D // 512):
                w = wbufs[wi_ % 2]
                wi_ += 1
                wv = self.load_panel(None, "o", w, [(n * 512, 512)], DC)
                for t in range(TPG):
                    ps = self.ps_next()
                    self.mm_tm(ps, wv, w.used, 0, 512, mix3, mixT.d, t * 128, DC)
                    x_ = xp[k % 2]
                    hs = hst[k % 2]
                    k += 1
                    r0 = tok0 + t * 128
                    self.dma("sp", x_.ap, self.xk[r0:r0 + 128, n * 512:(n + 1) * 512], writes=[x_.d])
                    S_.op("dve", lambda e, ps=ps, x_=x_, hs=hs: e.tensor_tensor(out=hs.ap, in0=x_.ap, in1=ps.ap[:, 0:512], op=ALU.add),
                          reads=[ps.d, x_.d], writes=[hs.d])
                    self.dma("pool", self.h1[r0:r0 + 128, n * 512:(n + 1) * 512], hs.ap, reads=[hs.d])

    def phase_ffn(self):
        c = self.cfg
        S_ = self.S
        D, DC, DFF = c.D, c.DC, c.DFF
        FC = DFF // 128
        GF = min(512, c.TOWN)
        TPF = GF // 128
        KP = c.KP
        PW1 = c.PW1
        NQ = 4 if FC >= 64 else 1
        FCQ = FC // NQ
        assert FCQ % KP == 0 and (FCQ * 128) % PW1 == 0
        self.reset_arena()
        hb = [self.tf32(D) for _ in range(TPF)]
        ub = self.tf32(D)
        u2mem = self.af32(max(DC * GF // 2, D))
        u2d = Dep()
        u2T3 = Tl(u2mem.bitcast(BF16)[:, 0:DC * GF].rearrange("p (c t) -> p c t", c=DC), u2d)
        gf = Tl(u2mem[:, 0:D], u2d)
        fT = self.tbf(FCQ * GF)
        fT3 = fT.ap.rearrange("p (c t) -> p c t", c=FCQ)
        wsz = max(DC * PW1, KP * 512)
        wbufs = [WBuf(self.abf(wsz)) for _ in range(2)]
        rl = [self.tf32(GF) for _ in range(2)]
        small = self.smalls()
        pacc = [self.ps[4 + t] for t in range(TPF)]
        pff = [self.ps[i] for i in range(4)]
        wi_ = 0
        fi = 0
        for g in range(c.TOWN // GF):
            tok0 = g * GF
            for t in range(TPF):
                self.dma("sp", hb[t].ap, self.h1[tok0 + t * 128:tok0 + (t + 1) * 128, :], writes=[hb[t].d])
                self.norm_transpose(hb[t], ub, ub, u2T3, t * 128, self.g2, small)
            for q in range(NQ):
                for n in range(q * FCQ * 128 // PW1, (q + 1) * FCQ * 128 // PW1):
                    w = wbufs[wi_ % 2]
                    wi_ += 1
                    wv = self.load_panel(None, "1", w, [(n * PW1, PW1)], DC)
                    for j in range(PW1 // 128):
                        fcl = n * (PW1 // 128) + j - q * FCQ
                        ps = pff[fi % 4]
                        fi += 1
                        self.mm_fm(ps, wv, w.used, j * 128, 128, u2T3.ap, u2T3.d, 0, GF, DC)
                        r = rl[fi % 2]
                        S_.op("act", lambda e, ps=ps, r=r: e.activation(out=r.ap, in_=ps.ap[:, 0:GF], func=AF.Relu), reads=[ps.d], writes=[r.d])
                        S_.op("dve", lambda e, r=r, fcl=fcl: e.tensor_tensor(out=fT3[:, fcl, :], in0=r.ap, in1=r.ap, op=ALU.mult), reads=[r.d], writes=[fT.d])
                if q == NQ - 1:
                    self.dma("sp", gf.ap, self.gf_in, writes=[u2d])
                for n in range(D // 512):
                    for kpl in range(FCQ // KP):
                        kp = q * (FCQ // KP) + kpl
                        w = wbufs[wi_ % 2]
                        wi_ += 1
                        wv = self.load_panel(None, "2", w, [(n * 512, 512)], KP, c0=kp * KP)
                        for t in range(TPF):
                            for cc in range(KP):
                                fcl = kpl * KP + cc
                                S_.op("pe", lambda e, t=t, cc=cc, fcl=fcl, wv=wv: e.matmul(pacc[t].ap[:, 0:512], lhsT=fT3[:, fcl, t * 128:(t + 1) * 128],
                                                                                        rhs=wv[:, cc, 0:512], start=(fcl == 0), stop=(fcl == FCQ - 1)),
                                      reads=[fT.d] + list(w.used), writes=[pacc[t].d])
                    for t in range(TPF):
                        S_.op("dve", lambda e, t=t, n=n: e.tensor_tensor(out=hb[t].ap[:, n * 512:(n + 1) * 512], in0=hb[t].ap[:, n * 512:(n + 1) * 512],
                                                                        in1=pacc[t].ap[:, 0:512], op=ALU.add),
                              reads=[pacc[t].d], writes=[hb[t].d])
            ss, ms, sd, rstd = small
            for t in range(TPF):
                S_.op("act", lambda e, t=t: e.activation(out=ub.ap, in_=hb[t].ap, func=AF.Square, accum_out=ss.ap), reads=[hb[t].d], writes=[ub.d, ss.d])
                S_.op("dve", lambda e: e.tensor_scalar(out=ms.ap, in0=ss.ap, scalar1=1.0 / D, scalar2=EPS, op0=ALU.mult, op1=ALU.add), reads=[ss.d], writes=[ms.d])
                S_.op("act", lambda e: e.sqrt(out=sd.ap, in_=ms.ap), reads=[ms.d], writes=[sd.d])
                S_.op("dve", lambda e: e.reciprocal(out=rstd.ap, in_=sd.ap), reads=[sd.d], writes=[rstd.d])
                S_.op("dve", lambda e, t=t: e.scalar_tensor_tensor(out=ub.ap, in0=hb[t].ap, scalar=rstd.ap, in1=gf.ap, op0=ALU.mult, op1=ALU.mult),
                      reads=[hb[t].d, rstd.d, u2d], writes=[ub.d])
                self.dma("sp", self.out[tok0 + t * 128:tok0 + (t + 1) * 128, :], ub.ap, reads=[ub.d])


def _t5_bucket_np(n):
    n = np.asarray(n)
    nf = np.maximum(n, 1).astype(np.float32)
    large = 16 + (np.log(nf / np.float32(16)) / np.float32(math.log(128 / 16)) * np.float32(16)).astype(np.int32)
    large = np.minimum(large, 31)
    return np.where(n < 16, n, large)


def pchunks(v, n):
    return np.ascontiguousarray(np.asarray(v, np.float32).reshape(n, 128).T)


def make_in_maps(cfg, x, norm1_g, w_in, b_gate, conv_w, conv_bias, conv_ln_g, conv_ln_b, w_conv_out, w_attn_out,
                 rel_bias, w_o, norm2_g, w_ff1, w_ff2, normf_g):
    c = cfg
    f32 = np.float32
    x = np.asarray(x, f32)
    B = x.shape[0]
    S, T = c.S, c.TOWN
    rel_bias = np.asarray(rel_bias, f32)
    s_idx = np.arange(128)[:, None]
    q_idx = np.arange(128)[None, :]
    bd = _t5_bucket_np(np.maximum(q_idx - s_idx, 0))
    bp = _t5_bucket_np(128 + q_idx - s_idx)
    ttab = np.zeros((128, c.H, 2, 128), f32)
    for h in range(c.H):
        ttab[:, h, 0, :] = rel_bias[bd, h]
        ttab[:, h, 1, :] = rel_bias[bp, h]
    shared = {
        "w_in": np.asarray(w_in, f32)[0], "w_co": np.asarray(w_conv_out, f32)[0], "w_ao": np.asarray(w_attn_out, f32)[0],
        "w_o": np.asarray(w_o, f32)[0], "w_1": np.asarray(w_ff1, f32)[0], "w_2": np.asarray(w_ff2, f32)[0],
        "g1": pchunks(np.asarray(norm1_g)[0], c.DC), "g2": pchunks(np.asarray(norm2_g)[0], c.DC),
        "gf": np.ascontiguousarray(np.broadcast_to(np.asarray(normf_g, f32)[None, :], (128, c.D))),
        "bg": pchunks(np.asarray(b_gate)[0], 2 * c.DC),
        "cw": np.ascontiguousarray(np.asarray(conv_w, f32)[0].T.reshape(c.CC, 128, 31).transpose(1, 0, 2).reshape(128, c.CC * 31)),
        "cb": pchunks(np.asarray(conv_bias)[0], c.CC), "lg": pchunks(np.asarray(conv_ln_g)[0], c.CC),
        "lb": pchunks(np.asarray(conv_ln_b)[0], c.CC),
        "rb31": np.ascontiguousarray(np.broadcast_to(rel_bias[31][None, :], (128, c.H))),
        "ttab": ttab.reshape(128, c.H * 2 * 128),
        "ident": np.eye(128, dtype=f32),
    }
    in_maps = []
    for core in range(4 * B):
        b, j = divmod(core, 4)
        own = np.arange(j * T, (j + 1) * T)
        if j > 0:
            halo = np.arange(j * T - 128, j * T)
            rest = np.concatenate([np.arange(0, j * T - 128), np.arange((j + 1) * T, S)])
            halo_pos = halo.astype(f32)
        else:
            halo = np.arange(S - 128, S)
            rest = np.arange(T, S - 128)
            halo_pos = np.full(128, 1e9, f32)
        order = np.concatenate([own, halo, rest])
        assert order.shape[0] == S
        xk = x[b][order]
        if j == 0:
            xk[T:T + 128] = 0.0
        kp = np.concatenate([own.astype(f32), halo_pos, rest.astype(f32)])
        m = dict(shared)
        m["xk"] = np.ascontiguousarray(xk)
        m["kpos"] = np.ascontiguousarray(np.broadcast_to(np.minimum(kp, 32767.0).astype(np.int16)[None, :], (128, S)))
        m["qpos"] = pchunks(own.astype(f32), T // 128)
        in_maps.append(m)
    return in_maps


_NC_CACHE = {}


def run(cfg, inputs, debug=False):
    key = (cfg.D, cfg.S, cfg.H, cfg.IH, cfg.DFF, cfg.TOPK, debug)
    if key not in _NC_CACHE:
        _NC_CACHE[key] = Builder(cfg, debug).build()
    nc = _NC_CACHE[key]
    in_maps = make_in_maps(cfg, **inputs)
    n = len(in_maps)
    res = run_bass_kernel_spmd(nc, in_maps, core_ids=list(range(n)))
    B = n // 4
    out = np.zeros((B, cfg.S, cfg.D), np.float32)
    for core in range(n):
        b, j = divmod(core, 4)
        out[b, j * cfg.TOWN:(j + 1) * cfg.TOWN] = res.results[core]["out"]
    return out, res


def kernel(**inputs):
    cfg = Cfg()
    out, _ = run(cfg, inputs)
    return out
```

```python
import contextlib
import math
import numpy as np
import concourse.bass as bass
import concourse.mybir as mybir
from concourse.bass_utils import run_bass_kernel_spmd

F32 = mybir.dt.float32
BF16 = mybir.dt.bfloat16
FP16 = mybir.dt.float16
I16 = mybir.dt.int16
AF = mybir.ActivationFunctionType
ALU = mybir.AluOpType
AX = mybir.AxisListType

ENGS = ("pe", "act", "dve", "pool", "sp")
DMA_RING = 24
EPS = 1e-6
NITER = 24
NEG = -1.0e30


class Cfg:
    def __init__(self, D=4096, S=8192, H=16, IH=32, DFF=16384, TOPK=256):
        self.D, self.S, self.H, self.IH, self.DFF, self.TOPK = D, S, H, IH, DFF, TOPK
        self.CH = D // 2
        self.HW = H * 128
        self.IW = IH * 64
        self.DC = D // 128
        self.CC = self.CH // 128
        self.TOWN = S // 4
        self.NSC = S // 128
        self.OFF_A = 0
        self.OFF_G = self.CH
        self.OFF_Q = 2 * self.CH
        self.OFF_K = self.OFF_Q + self.HW
        self.OFF_V = self.OFF_K + self.HW
        self.OFF_IQ = self.OFF_V + self.HW
        self.OFF_IK = self.OFF_IQ + self.IW
        self.OFF_IW = self.OFF_IK + 64
        self.OFF_GATE = self.OFF_IW + IH
        self.INW = self.OFF_GATE + 2 * D
        self.G = min(512, self.TOWN)
        self.QB = min(512, self.TOWN)
        self.GF = min(256, self.TOWN)
        self.HALO_CHUNK = self.TOWN // 128
        self.PW1 = 256
        self.KP = min(16, DFF // 128)


class Dep:
    __slots__ = ("w", "r", "ro")

    def __init__(self, ro=False):
        self.w = None
        self.r = {}
        self.ro = ro


class Sched:
    def __init__(self, nc, stack):
        self.nc = nc
        self.ops = {e: [] for e in ENGS}
        self.cnt = {e: 0 for e in ENGS}
        self.seen = {e: {} for e in ENGS}
        self.sem = {e: stack.enter_context(nc.semaphore("s_" + e)) for e in ENGS}
        self.ring = {}
        self.dma_n = {}
        for q in ("sp", "pool", "act"):
            self.ring[q] = [stack.enter_context(nc.semaphore(f"d_{q}{i}")) for i in range(DMA_RING)]
            self.dma_n[q] = 0
        self.nops = 0

    def _need(self, eng, waits, cid):
        if cid is None:
            return
        sem, val, src = cid
        if src == eng and src == "pe":
            return
        key = id(sem)
        if self.seen[eng].get(key, 0) >= val:
            return
        self.seen[eng][key] = val
        waits.append((sem, val))

    def op(self, eng, fn, reads=(), writes=(), dma=False):
        waits = []
        for d in reads:
            self._need(eng, waits, d.w)
        for d in writes:
            self._need(eng, waits, d.w)
            for r in d.r.values():
                self._need(eng, waits, r)
        if dma:
            n = self.dma_n[eng]
            self.dma_n[eng] = n + 1
            sem = self.ring[eng][n % DMA_RING]
            k = n // DMA_RING
            if k > 0:
                self._need(eng, waits, (sem, 16 * k, "dma"))
            cid = (sem, 16 * (k + 1), "dma")
            inc = (sem, 16)
            rkey = id(sem)
        else:
            self.cnt[eng] += 1
            cid = (self.sem[eng], self.cnt[eng], eng)
            inc = (self.sem[eng], 1)
            rkey = eng
        self.ops[eng].append((waits, fn, inc))
        self.nops += 1
        for d in reads:
            if not d.ro:
                d.r[rkey] = cid
        for d in writes:
            d.w = cid
            d.r = {}
        return cid

    def barrier(self):
        ids = []
        for e in ENGS:
            if self.cnt[e] > 0:
                ids.append((self.sem[e], self.cnt[e], e))
        for q in self.ring:
            n = self.dma_n[q]
            for i in range(DMA_RING):
                k = (n - i + DMA_RING - 1) // DMA_RING if n > i else 0
                if k > 0:
                    ids.append((self.ring[q][i], 16 * k, "dma"))
        for e in ENGS:
            waits = []
            for cid in ids:
                if cid[2] == e:
                    continue
                self._need(e, waits, cid)
            if waits:
                self.ops[e].append((waits, None, None))

    def emit(self, block):
        emap = {"pe": block.tensor, "act": block.scalar, "dve": block.vector, "pool": block.gpsimd, "sp": block.sync}
        for e in ENGS:
            ops = self.ops[e]

            def body(eng, ops=ops):
                for waits, fn, inc in ops:
                    for sem, val in waits:
                        eng.wait_ge(sem, val)
                    if fn is not None:
                        fn(eng).then_inc(inc[0], inc[1])

            emap[e](body)


class WBuf:
    __slots__ = ("ap", "ds", "used")

    def __init__(self, ap):
        self.ap = ap
        self.ds = [Dep() for _ in range(3)]
        self.used = []


class Tl:
    __slots__ = ("ap", "d")

    def __init__(self, ap, d=None):
        self.ap = ap
        self.d = d if d is not None else Dep()


ARENA_F32 = 47 * 1024


class Builder:
    def __init__(self, cfg, debug=False):
        self.cfg = cfg
        self.debug = debug

    def reset_arena(self):
        self.off = self.const_off

    def af32(self, n):
        a = self.off
        self.off += n
        assert self.off <= ARENA_F32, ("SBUF overflow", self.off)
        return self.arena[:, a:a + n]

    def abf(self, n):
        n2 = (n + 1) // 2
        return self.af32(n2).bitcast(BF16)[:, 0:n]

    def tf32(self, n):
        return Tl(self.af32(n))

    def tbf(self, n):
        return Tl(self.abf(n))

    def ps_next(self):
        b = self.ps_i % 8
        self.ps_i += 1
        return self.ps[b]

    def dma(self, q, out, in_, reads=(), writes=()):
        return self.S.op(q, lambda e: e.dma_start(out=out, in_=in_), reads=reads, writes=writes, dma=True)

    def build(self):
        c = self.cfg
        nc = bass.Bass("TRN2", target_bir_lowering=False)
        self.nc = nc
        D, S, H, IH, DFF = c.D, c.S, c.H, c.IH, c.DFF
        dt_in = lambda name, shape: nc.dram_tensor(name, shape, F32, kind="ExternalInput").ap()
        self.xk = dt_in("xk", [S, D])
        self.kpos_in = nc.dram_tensor("kpos", [128, S], I16, kind="ExternalInput").ap()
        self.qpos_in = dt_in("qpos", [128, c.TOWN // 128])
        self.w_in = dt_in("w_in", [D, c.INW])
        self.w_co = dt_in("w_co", [c.CH, D])
        self.w_ao = dt_in("w_ao", [c.HW, D])
        self.w_o = dt_in("w_o", [D, D])
        self.w_1 = dt_in("w_1", [D, DFF])
        self.w_2 = dt_in("w_2", [DFF, D])
        self.g1_in = dt_in("g1", [128, c.DC])
        self.g2_in = dt_in("g2", [128, c.DC])
        self.gf_in = dt_in("gf", [128, D])
        self.bg_in = dt_in("bg", [128, 2 * c.DC])
        self.cw_in = dt_in("cw", [128, c.CC * 31])
        self.cb_in = dt_in("cb", [128, c.CC])
        self.lg_in = dt_in("lg", [128, c.CC])
        self.lb_in = dt_in("lb", [128, c.CC])
        self.rb31_in = dt_in("rb31", [128, H])
        self.ttab_in = dt_in("ttab", [128, H * 2 * 128])
        self.ident_in = dt_in("ident", [128, 128])
        self.out = nc.dram_tensor("out", [c.TOWN, D], F32, kind="ExternalOutput").ap()
        kind = "ExternalOutput" if self.debug else "Internal"
        di = lambda name, shape, dt: nc.dram_tensor(name, shape, dt, kind=kind).ap()
        self.KT = di("KT", [H, 128, S], BF16)
        self.V = di("V", [H, 128, S], BF16)
        self.kiT = di("kiT", [128, S], FP16)
        self.QT = di("QT", [H, 128, c.TOWN], BF16)
        self.qiT = di("qiT", [IH // 2, 128, c.TOWN], FP16)
        self.wi = di("wi", [c.TOWN, IH], F32)
        self.gluT = di("gluT", [c.CC, 128, 32 + c.TOWN], F32)
        self.gatesT = di("gatesT", [2 * c.DC, 128, c.TOWN], F32)
        self.zT = di("zT", [c.CC, 128, c.TOWN], BF16)
        self.maskT = di("maskT", [c.NSC, 128, c.TOWN], BF16)
        self.oT = di("oT", [H, 128, c.TOWN], BF16)
        self.h1 = di("h1", [c.TOWN, D], F32)

        with contextlib.ExitStack() as st:
            self.arena = st.enter_context(nc.sbuf_tensor("arena", [128, ARENA_F32], F32))
            self.ps = [Tl(st.enter_context(nc.psum_tensor(f"ps{i}", [128, 512], F32))) for i in range(8)]
            self.ps_i = 0
            self.S = Sched(nc, st)
            self.off = 0
            self.const_off = 0
            self.consts()
            self.const_off = self.off
            self.setup_panels()
            self._cast_g = self.cast_gen(["in"])
            c_ = self.cfg
            n_up = ((c_.H + 3) // 4 + (c_.HW + 511) // 512 + 1)
            self.pump_casts(n_up * ((c_.DC + 7) // 8) + 2 * ((c_.DC + 7) // 8))
            self.phase_inproj()
            self.pump_casts(100000)
            self.S.barrier()
            self.phase_conv()
            self.S.barrier()
            self._cast_g = self.cast_gen(["co", "ao", "o"])
            self.phase_indexer()
            self.pump_casts(100000)
            self.S.barrier()
            self._cast_g = self.cast_gen(["1", "2"])
            self.phase_attn()
            self.pump_casts(100000)
            self.S.barrier()
            self.phase_mix()
            self.S.barrier()
            self.phase_ffn()
            self.S.barrier()
            with nc.Block() as block:
                self.S.emit(block)
        return nc

    def consts(self):
        c = self.cfg
        ld = lambda n, src: self._ld_const(n, src)
        self.ident = ld(128, self.ident_in)
        self.g1 = ld(c.DC, self.g1_in)
        self.g2 = ld(c.DC, self.g2_in)
        self.bg = ld(2 * c.DC, self.bg_in)
        self.cw = ld(c.CC * 31, self.cw_in)
        self.cb = ld(c.CC, self.cb_in)
        self.lg = ld(c.CC, self.lg_in)
        self.lb = ld(c.CC, self.lb_in)
        self.rb31 = ld(c.H, self.rb31_in)
        self.qpos = ld(c.TOWN // 128, self.qpos_in)
        self.ones32 = self.tf32(128)
        self.S.op("pool", lambda e: e.memset(self.ones32.ap, 1.0), writes=[self.ones32.d])
        self.zbf = self.tbf(512)
        self.S.op("pool", lambda e: e.memset(self.zbf.ap, 0.0), writes=[self.zbf.d])

    def _ld_const(self, n, src):
        t = self.tf32(n)
        self.dma("sp", t.ap, src, writes=[t.d])
        return t

    def panel_lists(self):
        c = self.cfg
        L = {}
        li = []
        for i in range(0, c.H, 4):
            li.append(([(c.OFF_K + i * 128, min(4, c.H - i) * 128)], c.DC, 0))
        for i in range(0, c.HW, 512):
            li.append(([(c.OFF_V + i, min(512, c.HW - i))], c.DC, 0))
        li.append(([(c.OFF_IK, 64), (c.OFF_IK, 64), (c.OFF_IW, c.IH)], c.DC, 0))
        for i in range(0, c.CC, 2):
            ncc = min(2, c.CC - i)
            li.append(([(c.OFF_A + i * 128, ncc * 128), (c.OFF_G + i * 128, ncc * 128)], c.DC, 0))
        for i in range(0, c.H, 4):
            li.append(([(c.OFF_Q + i * 128, min(4, c.H - i) * 128)], c.DC, 0))
        for i in range(0, c.IH // 2, 4):
            li.append(([(c.OFF_IQ + i * 128, min(4, c.IH // 2 - i) * 128)], c.DC, 0))
        for i in range(0, 2 * c.DC, 4):
            li.append(([(c.OFF_GATE + i * 128, 512)], c.DC, 0))
        L["in"] = li
        L["co"] = [([(n * 512, 512)], c.CC, 0) for n in range(c.D // 512)]
        L["ao"] = [([(n * 512, 512)], c.H, 0) for n in range(c.D // 512)]
        L["o"] = [([(n * 512, 512)], c.DC, 0) for n in range(c.D // 512)]
        L["1"] = [([(n * c.PW1, c.PW1)], c.DC, 0) for n in range(c.DFF // c.PW1)]
        L["2"] = [([(n * 512, 512)], c.KP, kp * c.KP) for n in range(c.D // 512) for kp in range(c.DFF // 128 // c.KP)]
        return L

    def setup_panels(self):
        self.plist = self.panel_lists()
        self.preg = {}
        self.wbp = {}
        self.pdeps = {}
        self.wsrc = {"in": self.w_in, "co": self.w_co, "ao": self.w_ao, "o": self.w_o, "1": self.w_1, "2": self.w_2}
        for name, li in self.plist.items():
            mx = max(kc * sum(w for _, w in segs) for segs, kc, c0 in li)
            self.wbp[name] = self.nc.dram_tensor("wbp_" + name, [len(li), 128, mx], BF16, kind="Internal").ap()
            for idx, (segs, kc, c0) in enumerate(li):
                self.preg[(name, tuple(segs), kc, c0)] = idx

    def cast_weights(self, names):
        for _ in self.cast_gen(names):
            pass

    def pump_casts(self, n):
        g = getattr(self, "_cast_g", None)
        if g is None:
            return
        for _ in range(n):
            try:
                next(g)
            except StopIteration:
                self._cast_g = None
                return

    def cast_gen(self, names):
        for name in names:
            src = self.wsrc[name]
            for idx, (segs, kc, c0) in enumerate(self.plist[name]):
                pw = sum(w for _, w in segs)
                dstv = self.wbp[name][idx, :, 0:kc * pw].rearrange("p (c w) -> p c w", c=kc)
                deps = []
                o = 0
                for col0, w in segs:
                    for cb in range(0, kc, 8):
                        ce = min(kc, cb + 8)
                        d = Dep(ro=True)
                        self.dma("pool", dstv[:, cb:ce, o:o + w],
                                 src[(c0 + cb) * 128:(c0 + ce) * 128, col0:col0 + w].rearrange("(c p) w -> p c w", p=128), writes=[d])
                        deps.append(d)
                        self.pdeps[(name, idx)] = deps
                        yield
                    o += w
                self.pdeps[(name, idx)] = deps

    def load_panel(self, wb, wname, buf, segs, kc, c0=0):
        idx = self.preg[(wname, tuple(segs), kc, c0)]
        pw = sum(w for _, w in segs)
        prev = list(buf.used)
        buf.used = [buf.ds[0]]
        self.dma("sp", buf.ap[:, 0:kc * pw], self.wbp[wname][idx, :, 0:kc * pw], reads=self.pdeps[(wname, idx)],
                 writes=[buf.ds[0]] + [d for d in prev if d is not buf.ds[0]])
        return buf.ap[:, 0:kc * pw].rearrange("p (c n) -> p c n", c=kc)

    def norm_transpose(self, xt, u, junk, uT3, col0, gam, small):
        c = self.cfg
        S_ = self.S
        D = c.D
        ss, ms, sd, rstd = small
        S_.op("act", lambda e: e.activation(out=junk.ap, in_=xt.ap, func=AF.Square, accum_out=ss.ap),
              reads=[xt.d], writes=[junk.d, ss.d])
        S_.op("dve", lambda e: e.tensor_scalar(out=ms.ap, in0=ss.ap, scalar1=1.0 / D, scalar2=EPS, op0=ALU.mult, op1=ALU.add),
              reads=[ss.d], writes=[ms.d])
        S_.op("act", lambda e: e.sqrt(out=sd.ap, in_=ms.ap), reads=[ms.d], writes=[sd.d])
        S_.op("dve", lambda e: e.reciprocal(out=rstd.ap, in_=sd.ap), reads=[sd.d], writes=[rstd.d])
        S_.op("act", lambda e: e.activation(out=u.ap, in_=xt.ap, func=AF.Copy, scale=rstd.ap),
              reads=[xt.d, rstd.d], writes=[u.d])
        for c4 in range(0, c.DC, 4):
            ps = self.ps_next()
            for j in range(4):
                cc = c4 + j
                S_.op("pe", lambda e, cc=cc, j=j, ps=ps: e.transpose(out=ps.ap[:, j * 128:(j + 1) * 128],
                                                                      in_=u.ap[:, cc * 128:(cc + 1) * 128], identity=self.ident.ap),
                      reads=[u.d, self.ident.d], writes=[ps.d])
            for j in range(4):
                cc = c4 + j
                if j % 2 == 0:
                    S_.op("dve", lambda e, cc=cc, j=j, ps=ps: e.tensor_scalar(out=uT3.ap[:, cc, col0:col0 + 128], in0=ps.ap[:, j * 128:(j + 1) * 128],
                                                                             scalar1=gam.ap[:, cc:cc + 1], scalar2=None, op0=ALU.mult),
                          reads=[ps.d, gam.d], writes=[uT3.d])
                else:
                    S_.op("act", lambda e, cc=cc, j=j, ps=ps: e.activation(out=uT3.ap[:, cc, col0:col0 + 128], in_=ps.ap[:, j * 128:(j + 1) * 128],
                                                                          func=AF.Copy, scale=gam.ap[:, cc:cc + 1]),
                          reads=[ps.d, gam.d], writes=[uT3.d])

    def smalls(self):
        return tuple(self.tf32(1) for _ in range(4))

    def mm_fm(self, ps, wv, wd, j0, jw, act3, actd, n0, n, kc, prow=None):
        for cc in range(kc):
            self.S.op("pe", lambda e, cc=cc: e.matmul(ps.ap[0:jw, 0:n], lhsT=wv[:, cc, j0:j0 + jw], rhs=act3[:, cc, n0:n0 + n],
                                                      start=(cc == 0), stop=(cc == kc - 1)),
                      reads=list(wd) + [actd], writes=[ps.d])

    def mm_tm(self, ps, wv, wd, j0, jw, act3, actd, t0, kc):
        for cc in range(kc):
            self.S.op("pe", lambda e, cc=cc: e.matmul(ps.ap[:, 0:jw], lhsT=act3[:, cc, t0:t0 + 128], rhs=wv[:, cc, j0:j0 + jw],
                                                      start=(cc == 0), stop=(cc == kc - 1)),
                      reads=list(wd) + [actd], writes=[ps.d])

    def phase_inproj(self):
        c = self.cfg
        S_ = self.S
        D, G, H, DC = c.D, c.G, c.H, c.DC
        NG = c.S // G
        OWNG = c.TOWN // G
        TPG = G // 128
        self.reset_arena()
        xts = [self.tf32(D) for _ in range(2)]
        junk = self.tbf(D)
        uT = self.tbf(DC * G)
        uT3 = Tl(uT.ap.rearrange("p (c t) -> p c t", c=DC), uT.d)
        wbufs = [WBuf(self.abf(DC * 512)) for _ in range(2)]
        smalls2 = [self.smalls() for _ in range(2)]
        xi = 0
        stg = [self.tbf(4 * G) for _ in range(3)]
        ki_st = Tl(self.af32((G + 1) // 2).bitcast(FP16)[:, 0:G])
        wi_st = self.tf32(TPG * c.IH)
        wi_st3 = wi_st.ap.rearrange("p (t n) -> p t n", t=TPG)
        glu_st = [self.tf32(2 * G) for _ in range(2)]
        sg_tmp = [self.tf32(G) for _ in range(2)]
        gate_st = [self.tf32(G) for _ in range(2)]
        wi_scale = (c.IH ** -0.5) * (64 ** -0.5)
        pi = [0]
        si = [0]

        def next_w():
            b = wbufs[pi[0] % 2]
            pi[0] += 1
            return b

        def next_s():
            b = stg[si[0] % 3]
            si[0] += 1
            return b

        def heads_fm(off, nunits, dst, evac_scale, evac_eng, slot0, f16=False):
            for i in range(0, nunits, 4):
                nh = min(4, nunits - i)
                wb = next_w()
                wv = self.load_panel(None, "in", wb, [(off + i * 128, nh * 128)], DC)
                st = next_s()
                st3 = (st.ap.bitcast(FP16) if f16 else st.ap).rearrange("p (h t) -> p h t", h=4)
                for j in range(nh):
                    ps = self.ps_next()
                    self.mm_fm(ps, wv, wb.used, j * 128, 128, uT3.ap, uT3.d, 0, G, DC)
                    if evac_eng == "act":
                        S_.op("act", lambda e, ps=ps, j=j, st3=st3: e.activation(out=st3[:, j, :], in_=ps.ap[:, 0:G], func=AF.Copy, scale=evac_scale),
                              reads=[ps.d], writes=[st.d])
                    else:
                        S_.op("dve", lambda e, ps=ps, j=j, st3=st3: e.tensor_copy(out=st3[:, j, :], in_=ps.ap[:, 0:G]),
                              reads=[ps.d], writes=[st.d])
                self.dma("pool", dst[i:i + nh, :, slot0:slot0 + G].rearrange("h p s -> p h s"), st3[:, 0:nh, :], reads=[st.d])

        order = [g for g in range(OWNG + 1, NG)] + [OWNG] + list(range(OWNG))
        n_rest = len(self.plist["in"]) * ((DC + 7) // 8) * 2
        per_g = (n_rest + max(1, NG - OWNG - 1) - 1) // max(1, NG - OWNG - 1)
        for g in order:
            own = g < OWNG
            halo = g == OWNG
            slot0 = g * G
            if not own:
                self.pump_casts(per_g)
            for t in range(TPG):
                xt = xts[xi % 2]
                small = smalls2[xi % 2]
                xi += 1
                self.dma("sp", xt.ap, self.xk[slot0 + t * 128: slot0 + (t + 1) * 128, :], writes=[xt.d])
                self.norm_transpose(xt, xt, junk, uT3, t * 128, self.g1, small)
            heads_fm(c.OFF_K, H, self.KT, 1.0, "act", slot0)
            for i in range(0, c.HW, 512):
                w = min(512, c.HW - i)
                wb = next_w()
                wv = self.load_panel(None, "in", wb, [(c.OFF_V + i, w)], DC)
                st = next_s()
                st3 = st.ap[:, 0:TPG * w].rearrange("p (t n) -> p t n", t=TPG)
                for t in range(TPG):
                    ps = self.ps_next()
                    self.mm_tm(ps, wv, wb.used, 0, w, uT3.ap, uT3.d, t * 128, DC)
                    S_.op("dve", lambda e, ps=ps, t=t, w=w, st3=st3: e.tensor_copy(out=st3[:, t, :], in_=ps.ap[:, 0:w]),
                          reads=[ps.d], writes=[st.d])
                for hh in range(w // 128):
                    self.dma("pool", self.V[i // 128 + hh, :, slot0:slot0 + G].rearrange("p (t d) -> p t d", t=TPG),
                             st3[:, :, hh * 128:(hh + 1) * 128], reads=[st.d])
            wb = next_w()
            wv = self.load_panel(None, "in", wb, [(c.OFF_IK, 64), (c.OFF_IK, 64), (c.OFF_IW, c.IH)], DC)
            ps = self.ps_next()
            self.mm_fm(ps, wv, wb.used, 0, 128, uT3.ap, uT3.d, 0, G, DC)
            S_.op("act", lambda e, ps=ps: e.copy(out=ki_st.ap, in_=ps.ap[:, 0:G]), reads=[ps.d], writes=[ki_st.d])
            self.dma("pool", self.kiT[:, slot0:slot0 + G], ki_st.ap, reads=[ki_st.d])
            if own:
                for t in range(TPG):
                    ps = self.ps_next()
                    self.mm_tm(ps, wv, wb.used, 128, c.IH, uT3.ap, uT3.d, t * 128, DC)
                    S_.op("act", lambda e, ps=ps, t=t: e.activation(out=wi_st3[:, t, :], in_=ps.ap[:, 0:c.IH], func=AF.Copy, scale=wi_scale),
                          reads=[ps.d], writes=[wi_st.d])
                self.dma("pool", self.wi[slot0:slot0 + G, :].rearrange("(t p) n -> p t n", p=128), wi_st3, reads=[wi_st.d])
            if own or halo:
                n0, n = (0, G) if own else (96, 32)
                dst0 = 32 + slot0 if own else 0
                for i in range(0, c.CC, 2):
                    ncc = min(2, c.CC - i)
                    wb = next_w()
                    wv = self.load_panel(None, "in", wb, [(c.OFF_A + i * 128, ncc * 128), (c.OFF_G + i * 128, ncc * 128)], DC)
                    gs = glu_st[(i // 2) % 2]
                    for j in range(ncc):
                        psa = self.ps_next()
                        self.mm_fm(psa, wv, wb.used, j * 128, 128, uT3.ap, uT3.d, n0, n, DC)
                        psg = self.ps_next()
                        self.mm_fm(psg, wv, wb.used, (ncc + j) * 128, 128, uT3.ap, uT3.d, n0, n, DC)
                        sg = sg_tmp[j % 2]
                        S_.op("act", lambda e, psg=psg, sg=sg, n=n: e.activation(out=sg.ap[:, 0:n], in_=psg.ap[:, 0:n], func=AF.Sigmoid),
                              reads=[psg.d], writes=[sg.d])
                        S_.op("dve", lambda e, psa=psa, sg=sg, gs=gs, j=j, n=n: e.tensor_tensor(out=gs.ap[:, j * G:j * G + n], in0=sg.ap[:, 0:n],
                                                                                      in1=psa.ap[:, 0:n], op=ALU.mult),
                              reads=[psa.d, sg.d], writes=[gs.d])
                    gv = gs.ap[:, 0:ncc * G].rearrange("p (c t) -> p c t", c=ncc)
                    self.dma("pool", self.gluT[i:i + ncc, :, dst0:dst0 + n].rearrange("c p s -> p c s"), gv[:, :, 0:n], reads=[gs.d])
            if own:
                heads_fm(c.OFF_Q, H, self.QT, 128 ** -0.5, "act", slot0)
                heads_fm(c.OFF_IQ, c.IH // 2, self.qiT, 1.0, "dve", slot0, f16=True)
                for i in range(0, 2 * DC, 4):
                    wb = next_w()
                    wv = self.load_panel(None, "in", wb, [(c.OFF_GATE + i * 128, 512)], DC)
                    for j in range(4):
                        f = i + j
                        gt = gate_st[f % 2]
                        ps = self.ps_next()
                        self.mm_fm(ps, wv, wb.used, j * 128, 128, uT3.ap, uT3.d, 0, G, DC)
                        S_.op("act", lambda e, ps=ps, f=f, gt=gt: e.activation(out=gt.ap, in_=ps.ap[:, 0:G],
                                                                               func=AF.Sigmoid, bias=self.bg.ap[:, f:f + 1]),
                              reads=[ps.d, self.bg.d], writes=[gt.d])
                        self.dma("pool", self.gatesT[f, :, slot0:slot0 + G], gt.ap, reads=[gt.d])

    def phase_conv(self):
        c = self.cfg
        S_ = self.S
        G, CC = c.G, c.CC
        self.reset_arena()
        glu = self.tf32(CC * (32 + G))
        glu3 = glu.ap.rearrange("p (c t) -> p c t", c=CC)
        y = self.tf32(CC * G)
        y3 = y.ap.rearrange("p (c t) -> p c t", c=CC)
        ysq = self.tf32(CC * G)
        ysq3 = ysq.ap.rearrange("p (c t) -> p c t", c=CC)
        mean, msq, var, rstd = (self.tf32(G) for _ in range(4))
        t1 = [self.tf32(G) for _ in range(2)]
        z_st = self.tbf(CC * G)
        z3 = z_st.ap.rearrange("p (c t) -> p c t", c=CC)
        cw3 = self.cw.ap.rearrange("p (c k) -> p c k", c=CC)
        for g in range(c.TOWN // G):
            tok0 = g * G
            self.dma("sp", glu3, self.gluT[:, :, tok0:tok0 + 32 + G].rearrange("c p s -> p c s"), writes=[glu.d])
            for cc in range(CC):
                S_.op("dve", lambda e, cc=cc: e.tensor_scalar(out=y3[:, cc, :], in0=glu3[:, cc, 2:2 + G], scalar1=cw3[:, cc, 0:1],
                                                            scalar2=self.cb.ap[:, cc:cc + 1], op0=ALU.mult, op1=ALU.add),
                      reads=[glu.d, self.cw.d, self.cb.d], writes=[y.d])
                for k in range(1, 31):
                    S_.op("dve", lambda e, cc=cc, k=k: e.scalar_tensor_tensor(out=y3[:, cc, :], in0=glu3[:, cc, 2 + k:2 + k + G],
                                                                             scalar=cw3[:, cc, k:k + 1], in1=y3[:, cc, :],
                                                                             op0=ALU.mult, op1=ALU.add),
                          reads=[glu.d, self.cw.d], writes=[y.d])
            S_.op("act", lambda e: e.activation(out=ysq.ap, in_=y.ap, func=AF.Square), reads=[y.d], writes=[ysq.d])
            ps_s = self.ps_next()
            for cc in range(CC):
                S_.op("pe", lambda e, cc=cc: e.matmul(ps_s.ap[:, 0:G], lhsT=self.ones32.ap, rhs=y3[:, cc, :], start=(cc == 0), stop=(cc == CC - 1)),
                      reads=[y.d, self.ones32.d], writes=[ps_s.d])
            ps_q = self.ps_next()
            for cc in range(CC):
                S_.op("pe", lambda e, cc=cc: e.matmul(ps_q.ap[:, 0:G], lhsT=self.ones32.ap, rhs=ysq3[:, cc, :], start=(cc == 0), stop=(cc == CC - 1)),
                      reads=[ysq.d, self.ones32.d], writes=[ps_q.d])
            S_.op("act", lambda e: e.activation(out=mean.ap, in_=ps_s.ap[:, 0:G], func=AF.Copy, scale=1.0 / c.CH), reads=[ps_s.d], writes=[mean.d])
            S_.op("dve", lambda e: e.tensor_tensor(out=msq.ap, in0=mean.ap, in1=mean.ap, op=ALU.mult), reads=[mean.d], writes=[msq.d])
            S_.op("dve", lambda e: e.scalar_tensor_tensor(out=var.ap, in0=ps_q.ap[:, 0:G], scalar=1.0 / c.CH, in1=msq.ap, op0=ALU.mult, op1=ALU.subtract),
                  reads=[ps_q.d, msq.d], writes=[var.d])
            S_.op("dve", lambda e: e.tensor_scalar(out=var.ap, in0=var.ap, scalar1=EPS, scalar2=None, op0=ALU.add), reads=[var.d], writes=[var.d])
            S_.op("act", lambda e: e.sqrt(out=msq.ap, in_=var.ap), reads=[var.d], writes=[msq.d])
            S_.op("dve", lambda e: e.reciprocal(out=rstd.ap, in_=msq.ap), reads=[msq.d], writes=[rstd.d])
            for cc in range(CC):
                tt = t1[cc % 2]
                S_.op("dve", lambda e, cc=cc, tt=tt: e.tensor_tensor(out=tt.ap, in0=y3[:, cc, :], in1=mean.ap, op=ALU.subtract),
                      reads=[y.d, mean.d], writes=[tt.d])
                S_.op("dve", lambda e, tt=tt: e.tensor_tensor(out=tt.ap, in0=tt.ap, in1=rstd.ap, op=ALU.mult), reads=[tt.d, rstd.d], writes=[tt.d])
                S_.op("act", lambda e, cc=cc, tt=tt: e.activation(out=z3[:, cc, :], in_=tt.ap, func=AF.Silu, bias=self.lb.ap[:, cc:cc + 1],
                                                                scale=self.lg.ap[:, cc:cc + 1]),
                      reads=[tt.d, self.lg.d, self.lb.d], writes=[z_st.d])
            self.dma("pool", self.zT[:, :, tok0:tok0 + G].rearrange("c p s -> p c s"), z3, reads=[z_st.d])

    def phase_indexer(self):
        c = self.cfg
        S_ = self.S
        S, IH, NSC = c.S, c.IH, c.NSC
        NHP = IH // 2
        NT = c.TOWN // 128
        LOOK = 3
        self.reset_arena()
        kiT = Tl(self.af32(S // 2).bitcast(FP16))
        kpos = Tl(self.af32(S // 2).bitcast(I16))
        penb = self.tbf(S)
        identbf = self.tbf(128)
        accs = [self.tf32(S) for _ in range(2)]
        mk = self.tf32(S)
        identb = Tl(self.af32(64).bitcast(FP16))
        rb = [Tl(self.af32(256).bitcast(FP16)) for _ in range(LOOK + 2)]
        qpad = Tl(self.af32(IH * 64).bitcast(FP16))
        qpad4 = qpad.ap.rearrange("p (h two q) -> p h two q", h=NHP, two=2)
        qpad3 = qpad.ap.rearrange("p (h q) -> p h q", h=IH)
        wi_t = [self.tf32(IH) for _ in range(2)]
        dgw = [Tl(self.af32(IH * 64).bitcast(FP16)) for _ in range(1)]
        absw = [self.tf32(IH) for _ in range(2)]
        sgnw = [self.tf32(IH) for _ in range(2)]
        MH = max(4, NSC // 2)
        mT_st = self.tbf(MH * 128)
        mT3 = mT_st.ap.rearrange("p (c q) -> p c q", c=MH)
        lo, hi, mid, cnt, ge, dd, nm = (self.tf32(1) for _ in range(7))
        pss = [self.ps[0], self.ps[1], self.ps[2], self.ps[3]]
        pacc = [self.ps[4], self.ps[5]]
        ptr = [self.ps[6], self.ps[7]]
        self.dma("sp", kiT.ap, self.kiT, writes=[kiT.d])
        self.dma("sp", kpos.ap, self.kpos_in, writes=[kpos.d])
        S_.op("dve", lambda e: e.tensor_copy(out=identb.ap, in_=self.ident.ap), reads=[self.ident.d], writes=[identb.d])
        S_.op("dve", lambda e: e.memset(qpad.ap, 0.0), writes=[qpad.d])
        S_.op("dve", lambda e: e.tensor_copy(out=identbf.ap, in_=self.ident.ap), reads=[self.ident.d], writes=[identbf.d])

        def unit_stream(t):
            acc = accs[t % 2]
            wt = wi_t[t % 2]
            dg = dgw[0]
            dg3 = dg.ap.rearrange("p (h q) -> p h q", h=IH)
            self.dma("sp", qpad4[0:64, :, 0, :], self.qiT[:, 0:64, t * 128:(t + 1) * 128].rearrange("h p q -> p h q"), writes=[qpad.d])
            qd2 = Dep()
            self.dma("sp", qpad4[64:128, :, 1, :], self.qiT[:, 64:128, t * 128:(t + 1) * 128].rearrange("h p q -> p h q"), reads=[qpad.d], writes=[qd2])
            self.dma("sp", wt.ap, self.wi[t * 128:(t + 1) * 128, :], writes=[wt.d])
            self.pump_casts(5)
            qp = self.qpos.ap[:, t:t + 1]
            aw = absw[t % 2]
            sg = sgnw[t % 2]
            S_.op("dve", lambda e, wt=wt, sg=sg: e.tensor_scalar(out=sg.ap, in0=wt.ap, scalar1=0.0, scalar2=2.0, op0=ALU.is_ge, op1=ALU.mult), reads=[wt.d], writes=[sg.d])
            S_.op("dve", lambda e, sg=sg: e.tensor_scalar(out=sg.ap, in0=sg.ap, scalar1=-1.0, scalar2=None, op0=ALU.add), reads=[sg.d], writes=[sg.d])
            S_.op("dve", lambda e, wt=wt, aw=aw, sg=sg: e.tensor_tensor(out=aw.ap, in0=wt.ap, in1=sg.ap, op=ALU.mult), reads=[wt.d, sg.d], writes=[aw.d])
            for h in range(IH):
                S_.op("dve", lambda e, h=h, wt=wt, dg3=dg3: e.tensor_scalar(out=dg3[:, h, :], in0=identb.ap, scalar1=wt.ap[:, h:h + 1], scalar2=None,
                                                                             op0=ALU.mult),
                      reads=[identb.d, wt.d], writes=[dg.d])
            S_.op("dve", lambda e, qp=qp: e.tensor_scalar(out=penb.ap, in0=kpos.ap, scalar1=qp, scalar2=NEG, op0=ALU.is_gt, op1=ALU.mult),
                  reads=[kpos.d, self.qpos.d], writes=[penb.d])
            yield
            units = [(s5i, h) for s5i in range(S // 512) for h in range(IH)]

            def A(u):
                s5i, h = units[u]
                hp, half = divmod(h, 2)
                r0 = half * 64
                ps = pss[u % 4]
                r = rb[u % len(rb)]
                S_.op("pe", lambda e, ps=ps, h=h, s5i=s5i: e.matmul(ps.ap[:, 0:512], lhsT=qpad3[:, h, :],
                                                                  rhs=kiT.ap[:, s5i * 512:(s5i + 1) * 512], start=True, stop=True),
                      reads=[qpad.d, qd2, kiT.d], writes=[ps.d])
                S_.op("act", lambda e, ps=ps, r=r: e.activation(out=r.ap, in_=ps.ap[:, 0:512], func=AF.Relu),
                      reads=[ps.d], writes=[r.d])

            def B(u):
                s5i, h = units[u]
                r = rb[u % len(rb)]
                pa = pacc[s5i % 2]
                if h == 0:
                    S_.op("pe", lambda e, pa=pa, s5i=s5i: e.matmul(pa.ap[:, 0:512], lhsT=identbf.ap, rhs=penb.ap[:, s5i * 512:(s5i + 1) * 512], start=True, stop=False),
                          reads=[penb.d, identbf.d], writes=[pa.d])
                S_.op("pe", lambda e, r=r, h=h, pa=pa: e.matmul(pa.ap[:, 0:512], lhsT=dg3[:, h, :], rhs=r.ap, start=False, stop=(h == IH - 1)),
                      reads=[r.d, dg.d], writes=[pa.d])
                if h == IH - 1:
                    S_.op("act", lambda e, pa=pa, s5i=s5i, acc=acc: e.copy(out=acc.ap[:, s5i * 512:(s5i + 1) * 512], in_=pa.ap[:, 0:512]),
                          reads=[pa.d], writes=[acc.d])

            n = len(units)
            for u in range(n + LOOK):
                if u < n:
                    A(u)
                if u - LOOK >= 0:
                    B(u - LOOK)
                yield

        def post_stream(t):
            acc = accs[t % 2]
            qp = self.qpos.ap[:, t:t + 1]
            S_.op("dve", lambda e, qp=qp: e.tensor_scalar(out=mk.ap, in0=kpos.ap, scalar1=qp, scalar2=-1.0, op0=ALU.is_le, op1=ALU.mult),
                  reads=[kpos.d, self.qpos.d], writes=[mk.d])
            yield
            S_.op("dve", lambda e, acc=acc: e.tensor_tensor(out=mk.ap, in0=mk.ap, in1=acc.ap, op=ALU.mult), reads=[acc.d], writes=[mk.d])
            S_.op("dve", lambda e: e.reduce_max(out=nm.ap, in_=mk.ap, axis=AX.X), reads=[mk.d], writes=[nm.d])
            yield
            S_.op("dve", lambda e: e.tensor_scalar(out=lo.ap, in0=nm.ap, scalar1=-1.0, scalar2=-1e-3, op0=ALU.mult, op1=ALU.add),
                  reads=[nm.d], writes=[lo.d])
            S_.op("dve", lambda e, acc=acc: e.reduce_max(out=hi.ap, in_=acc.ap, axis=AX.X), reads=[acc.d], writes=[hi.d])
            yield
            for it in range(NITER):
                S_.op("dve", lambda e: e.tensor_tensor(out=mid.ap, in0=lo.ap, in1=hi.ap, op=ALU.add), reads=[lo.d, hi.d], writes=[mid.d])
                S_.op("dve", lambda e: e.tensor_scalar(out=mid.ap, in0=mid.ap, scalar1=0.5, scalar2=None, op0=ALU.mult), reads=[mid.d], writes=[mid.d])
                S_.op("dve", lambda e, acc=acc: e.tensor_scalar(out=mk.ap, in0=acc.ap, scalar1=mid.ap, scalar2=None, op0=ALU.is_ge, op1=ALU.add, accum_out=cnt.ap),
                      reads=[acc.d, mid.d], writes=[mk.d, cnt.d])
                S_.op("dve", lambda e: e.tensor_single_scalar(out=ge.ap, in_=cnt.ap, scalar=c.TOPK - 0.5, op=ALU.is_ge), reads=[cnt.d], writes=[ge.d])
                S_.op("dve", lambda e: e.tensor_tensor(out=dd.ap, in0=mid.ap, in1=lo.ap, op=ALU.subtract), reads=[mid.d, lo.d], writes=[dd.d])
                S_.op("dve", lambda e: e.scalar_tensor_tensor(out=lo.ap, in0=dd.ap, scalar=ge.ap, in1=lo.ap, op0=ALU.mult, op1=ALU.add),
                      reads=[dd.d, ge.d], writes=[lo.d])
                S_.op("dve", lambda e: e.tensor_tensor(out=dd.ap, in0=hi.ap, in1=mid.ap, op=ALU.subtract), reads=[mid.d, hi.d], writes=[dd.d])
                S_.op("dve", lambda e: e.scalar_tensor_tensor(out=hi.ap, in0=dd.ap, scalar=ge.ap, in1=mid.ap, op0=ALU.mult, op1=ALU.add),
                      reads=[dd.d, ge.d, mid.d], writes=[hi.d])
                yield
            S_.op("dve", lambda e, acc=acc: e.tensor_scalar(out=mk.ap, in0=acc.ap, scalar1=lo.ap, scalar2=None, op0=ALU.is_ge), reads=[acc.d, lo.d], writes=[mk.d])
            yield "HOLD"
            for c4 in range(0, NSC, 4):
                ps = ptr[(c4 // 4) % 2]
                for j in range(4):
                    sc = c4 + j
                    S_.op("pe", lambda e, ps=ps, j=j, sc=sc: e.transpose(out=ps.ap[:, j * 128:(j + 1) * 128], in_=mk.ap[:, sc * 128:(sc + 1) * 128],
                                                                        identity=self.ident.ap),
                          reads=[mk.d, self.ident.d], writes=[ps.d])
                cl = c4 % MH
                S_.op("act", lambda e, ps=ps, cl=cl: e.copy(out=mT3[:, cl:cl + 4, :], in_=ps.ap.rearrange("p (c q) -> p c q", c=4)),
                      reads=[ps.d], writes=[mT_st.d])
                if cl + 4 == MH:
                    self.dma("pool", self.maskT[c4 + 4 - MH:c4 + 4, :, t * 128:(t + 1) * 128].rearrange("c p q -> p c q"), mT3, reads=[mT_st.d])
                yield

        for t in range(NT + 1):
            us = unit_stream(t) if t < NT else None
            pst = post_stream(t - 1) if t >= 1 else None
            nu = (S // 512) * IH + LOOK + 1
            npst = NITER + 5
            ratio = max(1, int(nu * 0.85) // (npst + 2))
            hold = False
            while us is not None or pst is not None:
                if us is not None:
                    for _ in range(ratio):
                        try:
                            next(us)
                        except StopIteration:
                            us = None
                            break
                if pst is not None and not (hold and us is not None):
                    try:
                        if next(pst) == "HOLD":
                            hold = True
                    except StopIteration:
                        pst = None

    def phase_attn(self):
        c = self.cfg
        S_ = self.S
        S, H, NSC, QB = c.S, c.H, c.NSC, c.QB
        QTP = QB // 128
        NPO = (QTP + 1) // 2
        LOOK = 3
        self.reset_arena()
        mqb = self.tbf(NSC * QB)
        mqb3 = mqb.ap.rearrange("p (c q) -> p c q", c=NSC)
        etab = self.tf32(H * 2 * 128)
        etab4 = etab.ap.rearrange("p (h a q) -> p h a q", h=H, a=2)
        KTh = [self.tbf(S) for _ in range(2)]
        QTh = [self.tbf(QB) for _ in range(2)]
        V1 = [self.tbf(NSC * 130) for _ in range(2)]
        pT = [self.tbf(QB) for _ in range(LOOK + 2)]
        o_sb = [self.tf32(128) for _ in range(2)]
        rden = [self.tf32(1) for _ in range(2)]
        oT_st = [self.tbf(QB) for _ in range(2)]
        nrb = self.tf32(H)
        self.dma("sp", etab.ap, self.ttab_in, writes=[etab.d])
        S_.op("dve", lambda e: e.tensor_scalar(out=nrb.ap, in0=self.rb31.ap, scalar1=-1.0, scalar2=None, op0=ALU.mult), reads=[self.rb31.d], writes=[nrb.d])
        for h in range(H):
            S_.op("act", lambda e, h=h: e.activation(out=etab4[:, h, :, :], in_=etab4[:, h, :, :], func=AF.Exp, bias=nrb.ap[:, h:h + 1]),
                  reads=[etab.d, nrb.d], writes=[etab.d])
        for v in V1:
            S_.op("dve", lambda e, v=v: e.memset(v.ap, 1.0), writes=[v.d])
        if NPO == 2:
            pos = [[self.ps[4], self.ps[5]], [self.ps[6], self.ps[7]]]
            pss = [self.ps[0], self.ps[1], self.ps[2]]
            pst = [self.ps[3]]
        else:
            pos = [[self.ps[4]], [self.ps[5]]]
            pss = [self.ps[0], self.ps[1], self.ps[2], self.ps[3]]
            pst = [self.ps[6], self.ps[7]]
        items = [(qb, h, sc) for qb in range(c.TOWN // QB) for h in range(H) for sc in range(NSC)]
        st = {}

        def stageA(i):
            qb, h, sc = items[i]
            hh = qb * H + h
            q0 = qb * QB
            kt, qt, v1 = KTh[hh % 2], QTh[hh % 2], V1[hh % 2]
            po = pos[hh % 2]
            v13 = v1.ap.rearrange("p (c d) -> p c d", c=NSC)
            if sc == 0:
                if h == 0:
                    self.dma("sp", mqb3, self.maskT[:, :, q0:q0 + QB].rearrange("c p q -> p c q"), writes=[mqb.d])
                self.pump_casts(7)
                self.dma("sp", kt.ap, self.KT[h, :, :], writes=[kt.d])
                self.dma("sp", qt.ap, self.QT[h, :, q0:q0 + QB], writes=[qt.d])
                self.dma("sp", v13[:, :, 0:128], self.V[h, :, :].rearrange("p (c d) -> p c d", c=NSC), writes=[v1.d])
                for b in range(NPO):
                    S_.op("pe", lambda e, b=b, po=po: e.matmul(po[b].ap[:, 0:512], lhsT=self.zbf.ap[:, 0:128], rhs=self.zbf.ap[:, 0:512], start=True, stop=False,
                                                               skip_group_check=True),
                          reads=[self.zbf.d], writes=[po[b].d])
            ps = pss[i % len(pss)]
            S_.op("pe", lambda e, ps=ps, sc=sc, kt=kt, qt=qt: e.matmul(ps.ap[:, 0:QB], lhsT=kt.ap[:, sc * 128:(sc + 1) * 128], rhs=qt.ap,
                                                                      start=True, stop=True),
                  reads=[kt.d, qt.d], writes=[ps.d])
            p = pT[i % len(pT)]
            S_.op("act", lambda e, ps=ps, p=p, h=h: e.activation(out=p.ap, in_=ps.ap[:, 0:QB], func=AF.Exp, bias=self.rb31.ap[:, h:h + 1]),
                  reads=[ps.d, self.rb31.d], writes=[p.d])
            S_.op("dve", lambda e, p=p, sc=sc: e.tensor_tensor(out=p.ap, in0=p.ap, in1=mqb3[:, sc, :], op=ALU.mult),
                  reads=[mqb.d], writes=[p.d])
            for tq in range(QTP):
                t = qb * QTP + tq
                pc = t - 1 if t >= 1 else c.HALO_CHUNK
                for a, chunk in ((0, t), (1, pc)):
                    if sc == chunk:
                        S_.op("pool", lambda e, p=p, tq=tq, a=a, h=h: e.tensor_tensor(out=p.ap[:, tq * 128:(tq + 1) * 128],
                                                                                     in0=p.ap[:, tq * 128:(tq + 1) * 128],
                                                                                     in1=etab4[:, h, a, :], op=ALU.mult),
                              reads=[etab.d], writes=[p.d])

        def stageB(i):
            qb, h, sc = items[i]
            hh = qb * H + h
            q0 = qb * QB
            v1 = V1[hh % 2]
            po = pos[hh % 2]
            ost = oT_st[hh % 2]
            v13 = v1.ap.rearrange("p (c d) -> p c d", c=NSC)
            p = pT[i % len(pT)]
            for tq in range(QTP):
                b, o = divmod(tq, 2)
                S_.op("pe", lambda e, p=p, tq=tq, b=b, o=o, sc=sc, v13=v13, po=po: e.matmul(po[b].ap[:, o * 129:o * 129 + 129],
                                                                                         lhsT=p.ap[:, tq * 128:(tq + 1) * 128], rhs=v13[:, sc, 0:129],
                                                                                         start=False, stop=(sc == NSC - 1), skip_group_check=True),
                      reads=[p.d, v1.d], writes=[po[b].d])
            if sc == NSC - 1:
                pt = pst[hh % len(pst)]
                for tq in range(QTP):
                    b, o = divmod(tq, 2)
                    rd = rden[tq % 2]
                    osb = o_sb[tq % 2]
                    S_.op("dve", lambda e, b=b, o=o, rd=rd, po=po: e.reciprocal(out=rd.ap, in_=po[b].ap[:, o * 129 + 128:o * 129 + 129]), reads=[po[b].d], writes=[rd.d])
                    S_.op("dve", lambda e, b=b, o=o, rd=rd, osb=osb, po=po: e.tensor_scalar(out=osb.ap, in0=po[b].ap[:, o * 129:o * 129 + 128], scalar1=rd.ap,
                                                                                          scalar2=None, op0=ALU.mult),
                          reads=[po[b].d, rd.d], writes=[osb.d])
                    S_.op("pe", lambda e, tq=tq, osb=osb, pt=pt: e.transpose(out=pt.ap[:, tq * 128:(tq + 1) * 128], in_=osb.ap, identity=self.ident.ap),
                          reads=[osb.d, self.ident.d], writes=[pt.d])
                S_.op("act", lambda e, pt=pt, ost=ost: e.copy(out=ost.ap, in_=pt.ap[:, 0:QB]), reads=[pt.d], writes=[ost.d])
                self.dma("pool", self.oT[h, :, q0:q0 + QB], ost.ap, reads=[ost.d])

        n = len(items)
        for i in range(n + LOOK):
            if i < n:
                stageA(i)
            if i - LOOK >= 0:
                stageB(i - LOOK)

    def phase_mix(self):
        c = self.cfg
        S_ = self.S
        D, G, DC, CC, H = c.D, c.G, c.DC, c.CC, c.H
        TPG = G // 128
        self.reset_arena()
        zt = self.tbf(CC * G)
        zt3 = zt.ap.rearrange("p (c t) -> p c t", c=CC)
        ot = self.tbf(H * G)
        ot3 = ot.ap.rearrange("p (c t) -> p c t", c=H)
        mixT = self.tbf(DC * G)
        mix3 = mixT.ap.rearrange("p (c t) -> p c t", c=DC)
        wbufs = [WBuf(self.abf(max(DC, CC + H) * 512)) for _ in range(2)]
        wi_ = 0
        ga = [self.tf32(G) for _ in range(2)]
        gb = [self.tf32(G) for _ in range(2)]
        m1 = [self.tf32(G) for _ in range(2)]
        m2 = [self.tf32(G) for _ in range(2)]
        xp = [self.tf32(512) for _ in range(2)]
        hst = [self.tf32(512) for _ in range(2)]
        k = 0
        for g in range(c.TOWN // G):
            tok0 = g * G
            self.dma("sp", zt3, self.zT[:, :, tok0:tok0 + G].rearrange("c p s -> p c s"), writes=[zt.d])
            self.dma("sp", ot3, self.oT[:, :, tok0:tok0 + G].rearrange("c p s -> p c s"), writes=[ot.d])
            for n in range(D // 512):
                wb = wbufs[wi_ % 2]
                wi_ += 1
                prev = list(wb.used)
                wb.used = [wb.ds[0], wb.ds[1]]
                wav = wb.ap[:, 0:CC * 512].rearrange("p (c n) -> p c n", c=CC)
                wbv = wb.ap[:, CC * 512:(CC + H) * 512].rearrange("p (c n) -> p c n", c=H)
                ia = self.preg[("co", ((n * 512, 512),), CC, 0)]
                ib = self.preg[("ao", ((n * 512, 512),), H, 0)]
                self.dma("sp", wb.ap[:, 0:CC * 512], self.wbp["co"][ia, :, 0:CC * 512], reads=self.pdeps[("co", ia)],
                         writes=[wb.ds[0]] + [d for d in prev if d is not wb.ds[0]])
                self.dma("sp", wb.ap[:, CC * 512:(CC + H) * 512], self.wbp["ao"][ib, :, 0:H * 512], reads=self.pdeps[("ao", ib)],
                         writes=[wb.ds[1]] + [d for d in prev if d is not wb.ds[1]])
                for j in range(4):
                    f = n * 4 + j
                    psa = self.ps_next()
                    self.mm_fm(psa, wav, [wb.ds[0]], j * 128, 128, zt3, zt.d, 0, G, CC)
                    psb = self.ps_next()
                    self.mm_fm(psb, wbv, [wb.ds[1]], j * 128, 128, ot3, ot.d, 0, G, H)
                    gA, gB, mA, mB = ga[k % 2], gb[k % 2], m1[k % 2], m2[k % 2]
                    k += 1
                    self.dma("sp", gA.ap, self.gatesT[f, :, tok0:tok0 + G], writes=[gA.d])
                    self.dma("sp", gB.ap, self.gatesT[DC + f, :, tok0:tok0 + G], writes=[gB.d])
                    S_.op("dve", lambda e, psa=psa, gA=gA, mA=mA: e.tensor_tensor(out=mA.ap, in0=gA.ap, in1=psa.ap[:, 0:G], op=ALU.mult),
                          reads=[psa.d, gA.d], writes=[mA.d])
                    S_.op("dve", lambda e, psb=psb, gB=gB, mB=mB: e.tensor_tensor(out=mB.ap, in0=gB.ap, in1=psb.ap[:, 0:G], op=ALU.mult),
                          reads=[psb.d, gB.d], writes=[mB.d])
                    S_.op("pool", lambda e, mA=mA, mB=mB, f=f: e.tensor_tensor(out=mix3[:, f, :], in0=mA.ap, in1=mB.ap, op=ALU.add),
                          reads=[mA.d, mB.d], writes=[mixT.d])
            for n in range(D // 512):
                w = wbufs[wi_ % 2]
                wi_ += 1
                wv = self.load_panel(None, "o", w, [(n * 512, 512)], DC)
                for t in range(TPG):
                    ps = self.ps_next()
                    self.mm_tm(ps, wv, w.used, 0, 512, mix3, mixT.d, t * 128, DC)
                    x_ = xp[k % 2]
                    hs = hst[k % 2]
                    k += 1
                    r0 = tok0 + t * 128
                    self.dma("sp", x_.ap, self.xk[r0:r0 + 128, n * 512:(n + 1) * 512], writes=[x_.d])
                    S_.op("dve", lambda e, ps=ps, x_=x_, hs=hs: e.tensor_tensor(out=hs.ap, in0=x_.ap, in1=ps.ap[:, 0:512], op=ALU.add),
                          reads=[ps.d, x_.d], writes=[hs.d])
                    self.dma("pool", self.h1[r0:r0 + 128, n * 512:(n + 1) * 512], hs.ap, reads=[hs.d])

    def phase_ffn(self):
        c = self.cfg
        S_ = self.S
        D, DC, DFF = c.D, c.DC, c.DFF
        FC = DFF // 128
        GF = min(512, c.TOWN)
        TPF = GF // 128
        KP = c.KP
        PW1 = c.PW1
        NQ = 4 if FC >= 64 else 1
        FCQ = FC // NQ
        assert FCQ % KP == 0 and (FCQ * 128) % PW1 == 0
        self.reset_arena()
        hb = [self.tf32(D) for _ in range(TPF)]
        ub = self.tf32(D)
        u2mem = self.af32(max(DC * GF // 2, D))
        u2d = Dep()
        u2T3 = Tl(u2mem.bitcast(BF16)[:, 0:DC * GF].rearrange("p (c t) -> p c t", c=DC), u2d)
        gf = Tl(u2mem[:, 0:D], u2d)
        fT = self.tbf(FCQ * GF)
        fT3 = fT.ap.rearrange("p (c t) -> p c t", c=FCQ)
        wsz = max(DC * PW1, KP * 512)
        wbufs = [WBuf(self.abf(wsz)) for _ in range(2)]
        rl = [self.tf32(GF) for _ in range(2)]
        small = self.smalls()
        pacc = [self.ps[4 + t] for t in range(TPF)]
        pff = [self.ps[i] for i in range(4)]
        wi_ = 0
        fi = 0
        for g in range(c.TOWN // GF):
            tok0 = g * GF
            for t in range(TPF):
                self.dma("sp", hb[t].ap, self.h1[tok0 + t * 128:tok0 + (t + 1) * 128, :], writes=[hb[t].d])
                self.norm_transpose(hb[t], ub, ub, u2T3, t * 128, self.g2, small)
            for q in range(NQ):
                for n in range(q * FCQ * 128 // PW1, (q + 1) * FCQ * 128 // PW1):
                    w = wbufs[wi_ % 2]
                    wi_ += 1
                    wv = self.load_panel(None, "1", w, [(n * PW1, PW1)], DC)
                    for j in range(PW1 // 128):
                        fcl = n * (PW1 // 128) + j - q * FCQ
                        ps = pff[fi % 4]
                        fi += 1
                        self.mm_fm(ps, wv, w.used, j * 128, 128, u2T3.ap, u2T3.d, 0, GF, DC)
                        r = rl[fi % 2]
                        S_.op("act", lambda e, ps=ps, r=r: e.activation(out=r.ap, in_=ps.ap[:, 0:GF], func=AF.Relu), reads=[ps.d], writes=[r.d])
                        S_.op("dve", lambda e, r=r, fcl=fcl: e.tensor_tensor(out=fT3[:, fcl, :], in0=r.ap, in1=r.ap, op=ALU.mult), reads=[r.d], writes=[fT.d])
                if q == NQ - 1:
                    self.dma("sp", gf.ap, self.gf_in, writes=[u2d])
                for n in range(D // 512):
                    for kpl in range(FCQ // KP):
                        kp = q * (FCQ // KP) + kpl
                        w = wbufs[wi_ % 2]
                        wi_ += 1
                        wv = self.load_panel(None, "2", w, [(n * 512, 512)], KP, c0=kp * KP)
                        for t in range(TPF):
                            for cc in range(KP):
                                fcl = kpl * KP + cc
                                S_.op("pe", lambda e, t=t, cc=cc, fcl=fcl, wv=wv: e.matmul(pacc[t].ap[:, 0:512], lhsT=fT3[:, fcl, t * 128:(t + 1) * 128],
                                                                                        rhs=wv[:, cc, 0:512], start=(fcl == 0), stop=(fcl == FCQ - 1)),
                                      reads=[fT.d] + list(w.used), writes=[pacc[t].d])
                    for t in range(TPF):
                        S_.op("dve", lambda e, t=t, n=n: e.tensor_tensor(out=hb[t].ap[:, n * 512:(n + 1) * 512], in0=hb[t].ap[:, n * 512:(n + 1) * 512],
                                                                        in1=pacc[t].ap[:, 0:512], op=ALU.add),
                              reads=[pacc[t].d], writes=[hb[t].d])
            ss, ms, sd, rstd = small
            for t in range(TPF):
                S_.op("act", lambda e, t=t: e.activation(out=ub.ap, in_=hb[t].ap, func=AF.Square, accum_out=ss.ap), reads=[hb[t].d], writes=[ub.d, ss.d])
                S_.op("dve", lambda e: e.tensor_scalar(out=ms.ap, in0=ss.ap, scalar1=1.0 / D, scalar2=EPS, op0=ALU.mult, op1=ALU.add), reads=[ss.d], writes=[ms.d])
                S_.op("act", lambda e: e.sqrt(out=sd.ap, in_=ms.ap), reads=[ms.d], writes=[sd.d])
                S_.op("dve", lambda e: e.reciprocal(out=rstd.ap, in_=sd.ap), reads=[sd.d], writes=[rstd.d])
                S_.op("dve", lambda e, t=t: e.scalar_tensor_tensor(out=ub.ap, in0=hb[t].ap, scalar=rstd.ap, in1=gf.ap, op0=ALU.mult, op1=ALU.mult),
                      reads=[hb[t].d, rstd.d, u2d], writes=[ub.d])
                self.dma("sp", self.out[tok0 + t * 128:tok0 + (t + 1) * 128, :], ub.ap, reads=[ub.d])


def _t5_bucket_np(n):
    n = np.asarray(n)
    nf = np.maximum(n, 1).astype(np.float32)
    large = 16 + (np.log(nf / np.float32(16)) / np.float32(math.log(128 / 16)) * np.float32(16)).astype(np.int32)
    large = np.minimum(large, 31)
    return np.where(n < 16, n, large)


def pchunks(v, n):
    return np.ascontiguousarray(np.asarray(v, np.float32).reshape(n, 128).T)


def make_in_maps(cfg, x, norm1_g, w_in, b_gate, conv_w, conv_bias, conv_ln_g, conv_ln_b, w_conv_out, w_attn_out,
                 rel_bias, w_o, norm2_g, w_ff1, w_ff2, normf_g):
    c = cfg
    f32 = np.float32
    x = np.asarray(x, f32)
    B = x.shape[0]
    S, T = c.S, c.TOWN
    rel_bias = np.asarray(rel_bias, f32)
    s_idx = np.arange(128)[:, None]
    q_idx = np.arange(128)[None, :]
    bd = _t5_bucket_np(np.maximum(q_idx - s_idx, 0))
    bp = _t5_bucket_np(128 + q_idx - s_idx)
    ttab = np.zeros((128, c.H, 2, 128), f32)
    for h in range(c.H):
        ttab[:, h, 0, :] = rel_bias[bd, h]
        ttab[:, h, 1, :] = rel_bias[bp, h]
    shared = {
        "w_in": np.asarray(w_in, f32)[0], "w_co": np.asarray(w_conv_out, f32)[0], "w_ao": np.asarray(w_attn_out, f32)[0],
        "w_o": np.asarray(w_o, f32)[0], "w_1": np.asarray(w_ff1, f32)[0], "w_2": np.asarray(w_ff2, f32)[0],
        "g1": pchunks(np.asarray(norm1_g)[0], c.DC), "g2": pchunks(np.asarray(norm2_g)[0], c.DC),
        "gf": np.ascontiguousarray(np.broadcast_to(np.asarray(normf_g, f32)[None, :], (128, c.D))),
        "bg": pchunks(np.asarray(b_gate)[0], 2 * c.DC),
        "cw": np.ascontiguousarray(np.asarray(conv_w, f32)[0].T.reshape(c.CC, 128, 31).transpose(1, 0, 2).reshape(128, c.CC * 31)),
        "cb": pchunks(np.asarray(conv_bias)[0], c.CC), "lg": pchunks(np.asarray(conv_ln_g)[0], c.CC),
        "lb": pchunks(np.asarray(conv_ln_b)[0], c.CC),
        "rb31": np.ascontiguousarray(np.broadcast_to(rel_bias[31][None, :], (128, c.H))),
        "ttab": ttab.reshape(128, c.H * 2 * 128),
        "ident": np.eye(128, dtype=f32),
    }
    in_maps = []
    for core in range(4 * B):
        b, j = divmod(core, 4)
        own = np.arange(j * T, (j + 1) * T)
        if j > 0:
            halo = np.arange(j * T - 128, j * T)
            rest = np.concatenate([np.arange(0, j * T - 128), np.arange((j + 1) * T, S)])
            halo_pos = halo.astype(f32)
        else:
            halo = np.arange(S - 128, S)
            rest = np.arange(T, S - 128)
            halo_pos = np.full(128, 1e9, f32)
        order = np.concatenate([own, halo, rest])
        assert order.shape[0] == S
        xk = x[b][order]
        if j == 0:
            xk[T:T + 128] = 0.0
        kp = np.concatenate([own.astype(f32), halo_pos, rest.astype(f32)])
        m = dict(shared)
        m["xk"] = np.ascontiguousarray(xk)
        m["kpos"] = np.ascontiguousarray(np.broadcast_to(np.minimum(kp, 32767.0).astype(np.int16)[None, :], (128, S)))
        m["qpos"] = pchunks(own.astype(f32), T // 128)
        in_maps.append(m)
    return in_maps


_NC_CACHE = {}


def run(cfg, inputs, debug=False):
    key = (cfg.D, cfg.S, cfg.H, cfg.IH, cfg.DFF, cfg.TOPK, debug)
    if key not in _NC_CACHE:
        _NC_CACHE[key] = Builder(cfg, debug).build()
    nc = _NC_CACHE[key]
    in_maps = make_in_maps(cfg, **inputs)
    n = len(in_maps)
    res = run_bass_kernel_spmd(nc, in_maps, core_ids=list(range(n)))
    B = n // 4
    out = np.zeros((B, cfg.S, cfg.D), np.float32)
    for core in range(n):
        b, j = divmod(core, 4)
        out[b, j * cfg.TOWN:(j + 1) * cfg.TOWN] = res.results[core]["out"]
    return out, res


def kernel(**inputs):
    cfg = Cfg()
    out, _ = run(cfg, inputs)
    return out
```
